# Optimizing a Trainium2 kernel written in Bass

```python
import math
import jax, jax.numpy as jnp
from jax import lax
import numpy as np

D_MODEL = 4096
BATCH = 2
SEQ = 4096
DEPTH = 4

N_MIXERS = 2
EPS = 1e-5

SSM_EXPAND = 2
D_INNER = SSM_EXPAND * D_MODEL
SSM_HEADDIM = 64
SSM_HEADS = D_INNER // SSM_HEADDIM
SSM_GROUPS = 8
HEADS_PER_GROUP = SSM_HEADS // SSM_GROUPS
SSM_STATE = 128
SSM_CONV = 4
CONV_DIM = D_INNER + 2 * SSM_GROUPS * SSM_STATE
CHUNK = 128
DT_MIN = 1e-3
DT_MAX = 1e-1

DIFF_HEAD_DIM = 128
DIFF_HEADS = D_MODEL // (2 * DIFF_HEAD_DIM)
Q_BLOCK = 128
ATTN_SCALE = DIFF_HEAD_DIM ** -0.5
ROPE_THETA = 10000.0

D_FF = 256 * ((8 * D_MODEL // 3 + 255) // 256)
FFN_CONV = 3

kernel_name = "hybrid_ssd_diffattn_convffn_trunk"


def rmsnorm(x, w):
    xf = x.astype(jnp.float32)
    y = xf * lax.rsqrt(jnp.mean(xf * xf, axis=-1, keepdims=True) + EPS)
    return (y * w.astype(jnp.float32)).astype(x.dtype)


def causal_dwconv(x, w, b):
    k_width = w.shape[0]
    length = x.shape[1]
    xp = jnp.pad(x, ((0, 0), (k_width - 1, 0), (0, 0)))
    return b + sum(w[k] * xp[:, k:k + length] for k in range(k_width))


def gated_group_rmsnorm(y, z, w):
    b, l, _ = y.shape
    g = (y.astype(jnp.float32) * jax.nn.silu(z.astype(jnp.float32))).reshape(b, l, SSM_GROUPS, -1)
    g = g * lax.rsqrt(jnp.mean(g * g, axis=-1, keepdims=True) + EPS)
    return (g.reshape(b, l, D_INNER) * w.astype(jnp.float32)).astype(y.dtype)


def ssd_chunked(x, a, bm, cm):
    b, l, g, h, p = x.shape
    n = bm.shape[-1]
    c = l // CHUNK
    x = x.reshape(b, c, CHUNK, g, h, p)
    bm = bm.reshape(b, c, CHUNK, g, n)
    cm = cm.reshape(b, c, CHUNK, g, n)
    a = a.reshape(b, c, CHUNK, g, h).transpose(0, 3, 4, 1, 2)
    a_cs = jnp.cumsum(a, axis=-1)
    causal = jnp.tri(CHUNK, dtype=bool)
    seg = a_cs[..., :, None] - a_cs[..., None, :]
    decay_in = jnp.exp(jnp.where(causal, seg, -jnp.inf))
    cb = jnp.einsum("bclgn,bcsgn->bcgls", cm, bm)
    y_diag = jnp.einsum("bcgls,bghcls,bcsghp->bclghp", cb, decay_in, x)
    decay_states = jnp.exp(a_cs[..., -1:] - a_cs)
    states = jnp.einsum("bcsgn,bghcs,bcsghp->bcghpn", bm, decay_states, x)
    chunk_cs = jnp.cumsum(jnp.pad(a_cs[..., -1], ((0, 0), (0, 0), (0, 0), (1, 0))), axis=-1)
    decay_chunk = jnp.exp(jnp.where(jnp.tri(c + 1, dtype=bool),
                                    chunk_cs[..., :, None] - chunk_cs[..., None, :], -jnp.inf))
    states = jnp.concatenate([jnp.zeros_like(states[:, :1]), states], axis=1)
    states = jnp.einsum("bghzc,bcghpn->bzghpn", decay_chunk, states)[:, :-1]
    y_off = jnp.einsum("bclgn,bcghpn,bghcl->bclghp", cm, states, jnp.exp(a_cs))
    return (y_diag + y_off).reshape(b, l, g, h, p)


def mamba2_mixer(u, w_in, conv_w, conv_b, dt_bias, a_log, d_skip, norm_w, w_out):
    b, l, _ = u.shape
    zxbcdt = u @ w_in
    z = zxbcdt[..., :D_INNER]
    xbc = zxbcdt[..., D_INNER:D_INNER + CONV_DIM]
    dt = zxbcdt[..., D_INNER + CONV_DIM:]
    xbc = jax.nn.silu(causal_dwconv(xbc, conv_w, conv_b))
    xs = xbc[..., :D_INNER].reshape(b, l, SSM_GROUPS, HEADS_PER_GROUP, SSM_HEADDIM)
    bm = xbc[..., D_INNER:D_INNER + SSM_GROUPS * SSM_STATE].reshape(b, l, SSM_GROUPS, SSM_STATE)
    cm = xbc[..., D_INNER + SSM_GROUPS * SSM_STATE:].reshape(b, l, SSM_GROUPS, SSM_STATE)
    dt = jax.nn.softplus(dt.astype(jnp.float32) + dt_bias.astype(jnp.float32))
    dt = dt.reshape(b, l, SSM_GROUPS, HEADS_PER_GROUP)
    a = -jnp.exp(a_log.astype(jnp.float32)).reshape(SSM_GROUPS, HEADS_PER_GROUP)
    xf = xs.astype(jnp.float32)
    y = ssd_chunked(xf * dt[..., None], dt * a, bm.astype(jnp.float32), cm.astype(jnp.float32))
    y = y + d_skip.astype(jnp.float32).reshape(SSM_GROUPS, HEADS_PER_GROUP)[:, :, None] * xf
    y = y.reshape(b, l, D_INNER).astype(u.dtype)
    return gated_group_rmsnorm(y, z, norm_w) @ w_out


def rotary_tables(positions):
    inv_freq = 1.0 / (ROPE_THETA ** (jnp.arange(0, DIFF_HEAD_DIM, 2, dtype=jnp.float32) / DIFF_HEAD_DIM))
    freqs = positions.astype(jnp.float32)[..., None] * inv_freq
    emb = jnp.concatenate([freqs, freqs], axis=-1)
    return jnp.cos(emb), jnp.sin(emb)


def apply_rope(t, cos, sin):
    c = cos[:, :, None, None, :]
    s = sin[:, :, None, None, :]
    half = DIFF_HEAD_DIM // 2
    tf = t.astype(jnp.float32)
    rot = jnp.concatenate([-tf[..., half:], tf[..., :half]], axis=-1)
    return (tf * c + rot * s).astype(t.dtype)


def diff_attention(u, cos, sin, w_qkv, lq1, lk1, lq2, lk2, subln_w, w_o, lambda_init):
    b, l, _ = u.shape
    qkv = u @ w_qkv
    q = qkv[..., :D_MODEL].reshape(b, l, DIFF_HEADS, 2, DIFF_HEAD_DIM)
    k = qkv[..., D_MODEL:2 * D_MODEL].reshape(b, l, DIFF_HEADS, 2, DIFF_HEAD_DIM)
    v = qkv[..., 2 * D_MODEL:].reshape(b, l, DIFF_HEADS, 2 * DIFF_HEAD_DIM)
    q = apply_rope(q, cos, sin)
    k = apply_rope(k, cos, sin)
    lam = (jnp.exp(jnp.sum(lq1.astype(jnp.float32) * lk1.astype(jnp.float32)))
           - jnp.exp(jnp.sum(lq2.astype(jnp.float32) * lk2.astype(jnp.float32))) + lambda_init)
    k = k.transpose(0, 2, 3, 1, 4)
    v = v.transpose(0, 2, 1, 3)
    n_blk = l // Q_BLOCK
    qb = q.reshape(b, n_blk, Q_BLOCK, DIFF_HEADS, 2, DIFF_HEAD_DIM).transpose(1, 0, 3, 4, 2, 5)
    k_pos = jnp.arange(l)

    def one_block(args):
        q_blk, blk = args
        s = jnp.einsum("bhcqd,bhckd->bhcqk", q_blk, k).astype(jnp.float32) * ATTN_SCALE
        q_pos = blk * Q_BLOCK + jnp.arange(Q_BLOCK)
        s = jnp.where(k_pos[None, :] <= q_pos[:, None], s, -jnp.inf)
        p = jax.nn.softmax(s, axis=-1)
        attn = p[:, :, 0] - lam * p[:, :, 1]
        return jnp.einsum("bhqk,bhkv->bhqv", attn.astype(v.dtype), v)

    o = lax.map(one_block, (qb, jnp.arange(n_blk)))
    o = rmsnorm(o, subln_w) * (1.0 - lambda_init)
    o = o.transpose(1, 0, 3, 2, 4).reshape(b, l, DIFF_HEADS * 2 * DIFF_HEAD_DIM)
    return o @ w_o


def conv_ffn(u, w_up, conv_w, conv_b, w_down):
    h = causal_dwconv(u @ w_up, conv_w, conv_b)
    return (jax.nn.silu(h[..., :D_FF]) * h[..., D_FF:]) @ w_down


def setup_inputs(seed: int = 0) -> dict:
    key = jax.random.key(seed)
    keys = list(jax.random.split(key, 96))
    kit = iter(keys)

    def nrm(shape, scale):
        return jax.random.normal(next(kit), shape, jnp.float32) * scale

    def gain(n):
        return 1.0 + nrm((n,), 0.02)

    inputs = {}
    inputs["x"] = nrm((BATCH, SEQ, D_MODEL), 1.0)
    inputs["positions"] = (jnp.arange(SEQ, dtype=jnp.int32)[None, :]
                           + jax.random.randint(next(kit), (BATCH, 1), 0, 1024, dtype=jnp.int32))
    for i in range(DEPTH):
        p = f"l{i}_"
        inputs[p + "norm_mix"] = gain(D_MODEL)
        if i % N_MIXERS == 0:
            inputs[p + "m_w_in"] = nrm((D_MODEL, 2 * D_INNER + 2 * SSM_GROUPS * SSM_STATE + SSM_HEADS), D_MODEL ** -0.5)
            inputs[p + "m_conv_w"] = nrm((SSM_CONV, CONV_DIM), SSM_CONV ** -0.5)
            inputs[p + "m_conv_b"] = nrm((CONV_DIM,), 0.02)
            dt = jnp.exp(jax.random.uniform(next(kit), (SSM_HEADS,), jnp.float32)
                         * (math.log(DT_MAX) - math.log(DT_MIN)) + math.log(DT_MIN))
            inputs[p + "m_dt_bias"] = dt + jnp.log(-jnp.expm1(-dt))
            inputs[p + "m_a_log"] = jnp.log(jax.random.uniform(next(kit), (SSM_HEADS,), jnp.float32, 1.0, 16.0))
            inputs[p + "m_d"] = 1.0 + nrm((SSM_HEADS,), 0.1)
            inputs[p + "m_norm"] = gain(D_INNER)
            inputs[p + "m_w_out"] = nrm((D_INNER, D_MODEL), D_INNER ** -0.5)
        else:
            inputs[p + "a_w_qkv"] = nrm((D_MODEL, 3 * D_MODEL), D_MODEL ** -0.5)
            inputs[p + "a_lq1"] = nrm((DIFF_HEAD_DIM,), 0.1)
            inputs[p + "a_lk1"] = nrm((DIFF_HEAD_DIM,), 0.1)
            inputs[p + "a_lq2"] = nrm((DIFF_HEAD_DIM,), 0.1)
            inputs[p + "a_lk2"] = nrm((DIFF_HEAD_DIM,), 0.1)
            inputs[p + "a_subln"] = gain(2 * DIFF_HEAD_DIM)
            inputs[p + "a_w_o"] = nrm((D_MODEL, D_MODEL), D_MODEL ** -0.5)
        inputs[p + "norm_ffn"] = gain(D_MODEL)
        inputs[p + "f_w_up"] = nrm((D_MODEL, 2 * D_FF), D_MODEL ** -0.5)
        inputs[p + "f_conv_w"] = nrm((FFN_CONV, 2 * D_FF), FFN_CONV ** -0.5)
        inputs[p + "f_conv_b"] = nrm((2 * D_FF,), 0.02)
        inputs[p + "f_w_down"] = nrm((D_FF, D_MODEL), D_FF ** -0.5)
    inputs["final_norm"] = gain(D_MODEL)
    return inputs


def reference(x, positions,
              l0_norm_mix, l0_m_w_in, l0_m_conv_w, l0_m_conv_b, l0_m_dt_bias, l0_m_a_log, l0_m_d, l0_m_norm, l0_m_w_out,
              l0_norm_ffn, l0_f_w_up, l0_f_conv_w, l0_f_conv_b, l0_f_w_down,
              l1_norm_mix, l1_a_w_qkv, l1_a_lq1, l1_a_lk1, l1_a_lq2, l1_a_lk2, l1_a_subln, l1_a_w_o,
              l1_norm_ffn, l1_f_w_up, l1_f_conv_w, l1_f_conv_b, l1_f_w_down,
              l2_norm_mix, l2_m_w_in, l2_m_conv_w, l2_m_conv_b, l2_m_dt_bias, l2_m_a_log, l2_m_d, l2_m_norm, l2_m_w_out,
              l2_norm_ffn, l2_f_w_up, l2_f_conv_w, l2_f_conv_b, l2_f_w_down,
              l3_norm_mix, l3_a_w_qkv, l3_a_lq1, l3_a_lk1, l3_a_lq2, l3_a_lk2, l3_a_subln, l3_a_w_o,
              l3_norm_ffn, l3_f_w_up, l3_f_conv_w, l3_f_conv_b, l3_f_w_down,
              final_norm):
    norm_mix = [l0_norm_mix, l1_norm_mix, l2_norm_mix, l3_norm_mix]
    norm_ffn = [l0_norm_ffn, l1_norm_ffn, l2_norm_ffn, l3_norm_ffn]
    mixer_params = [
        (l0_m_w_in, l0_m_conv_w, l0_m_conv_b, l0_m_dt_bias, l0_m_a_log, l0_m_d, l0_m_norm, l0_m_w_out),
        (l1_a_w_qkv, l1_a_lq1, l1_a_lk1, l1_a_lq2, l1_a_lk2, l1_a_subln, l1_a_w_o),
        (l2_m_w_in, l2_m_conv_w, l2_m_conv_b, l2_m_dt_bias, l2_m_a_log, l2_m_d, l2_m_norm, l2_m_w_out),
        (l3_a_w_qkv, l3_a_lq1, l3_a_lk1, l3_a_lq2, l3_a_lk2, l3_a_subln, l3_a_w_o),
    ]
    ffn_params = [
        (l0_f_w_up, l0_f_conv_w, l0_f_conv_b, l0_f_w_down),
        (l1_f_w_up, l1_f_conv_w, l1_f_conv_b, l1_f_w_down),
        (l2_f_w_up, l2_f_conv_w, l2_f_conv_b, l2_f_w_down),
        (l3_f_w_up, l3_f_conv_w, l3_f_conv_b, l3_f_w_down),
    ]
    cos, sin = rotary_tables(positions)
    for i in range(DEPTH):
        u = rmsnorm(x, norm_mix[i])
        if i % N_MIXERS == 0:
            x = x + mamba2_mixer(u, *mixer_params[i])
        else:
            lambda_init = 0.8 - 0.6 * math.exp(-0.3 * i)
            x = x + diff_attention(u, cos, sin, *mixer_params[i], lambda_init)
        x = x + conv_ffn(rmsnorm(x, norm_ffn[i]), *ffn_params[i])
    return rmsnorm(x, final_norm)
```

```python
import math
from contextlib import ExitStack

import numpy as np
import ml_dtypes

import concourse.bass as bass
import concourse.mybir as mybir
from concourse.bass_utils import run_bass_kernel_spmd

F32 = mybir.dt.float32
BF16 = mybir.dt.bfloat16
AF = mybir.ActivationFunctionType
ALU = mybir.AluOpType
NPBF = ml_dtypes.bfloat16

D = 4096
SEQ = 4096
NB = 2
DEPTH = 4
EPS = 1e-5
DFF = 11008
NFB = DFF // 128
DIN = 8192
TL = 1024
HALO = 2
TLH = TL + HALO
TT3 = [(0, 342), (342, 342), (684, 342)]
GROUPS = [22, 22, 21, 21]

ENGS = ["pe", "act", "dve", "pool", "sp"]
NDMA = {"sp": 12, "pool": 6, "act": 4}


class Prog:
    def __init__(self, nc, es):
        self.nc = nc
        self.sem = {}
        for e in ENGS:
            self.sem[("e", e)] = es.enter_context(nc.semaphore("s_" + e))
        for q, n in NDMA.items():
            for i in range(n):
                self.sem[("d", q, i)] = es.enter_context(nc.semaphore("d_%s%d" % (q, i)))
        self.sem[("ph",)] = es.enter_context(nc.semaphore("s_phase"))
        self.sem[("cc",)] = es.enter_context(nc.semaphore("s_cc"))
        self.ncc = 0
        self.cnt = {e: 0 for e in ENGS}
        self.dval = {k: 0 for k in self.sem if k[0] == "d"}
        self.ndma = {q: 0 for q in NDMA}
        self.nphase = 0
        self._reset()

    def _reset(self):
        self.ops = {e: [] for e in ENGS}
        self.state = {}
        self.waited = {e: {} for e in ENGS}

    def _deps(self, eng, reads, writes):
        need = {}

        def add(ev):
            if ev is None:
                return
            k, v = ev
            if need.get(k, 0) < v:
                need[k] = v

        for key in reads:
            st = self.state.get(key)
            if st:
                add(st[0])
        for key in writes:
            st = self.state.get(key)
            if st:
                add(st[0])
                for k, v in st[1].items():
                    add((k, v))
        out = []
        w = self.waited[eng]
        for k, v in need.items():
            if w.get(k, 0) < v:
                w[k] = v
                out.append((k, v))
        return out

    def _mark(self, ev, reads, writes):
        for key in reads:
            st = self.state.setdefault(key, [None, {}])
            if st[1].get(ev[0], 0) < ev[1]:
                st[1][ev[0]] = ev[1]
        for key in writes:
            self.state[key] = [ev, {}]

    def op(self, eng, fn, reads=(), writes=()):
        waits = self._deps(eng, reads, writes)
        self.cnt[eng] += 1
        ev = (("e", eng), self.cnt[eng])
        self.ops[eng].append((waits, fn, ev[0], 1))
        self._mark(ev, reads, writes)

    def dma(self, q, fn, reads=(), writes=()):
        waits = self._deps(q, reads, writes)
        i = self.ndma[q] % NDMA[q]
        self.ndma[q] += 1
        sk = ("d", q, i)
        prev = self.dval[sk]
        if prev > 0 and self.waited[q].get(sk, 0) < prev:
            self.waited[q][sk] = prev
            waits.append((sk, prev))
        self.dval[sk] = prev + 16
        ev = (sk, prev + 16)
        self.ops[q].append((waits, fn, sk, 16))
        self._mark(ev, reads, writes)

    def cc(self, fn, reads=(), writes=()):
        waits = self._deps("pool", reads, writes)
        self.ncc += 1
        ev = (("cc",), self.ncc)
        self.ops["pool"].append((waits, fn, ("cc",), 1))
        self._mark(ev, reads, writes)

    def flush(self):
        nc = self.nc
        self.nphase += 1
        ph = self.nphase
        final_cnt = dict(self.cnt)
        final_d = dict(self.dval)
        final_cc = self.ncc
        sem = self.sem
        ops = self.ops

        def run(engname, eng):
            for waits, fn, sk, inc in ops[engname]:
                for k, v in waits:
                    eng.wait_ge(sem[k], v)
                inst = fn(eng)
                inst.then_inc(sem[sk], inc)
            if engname == "sp":
                for e in ENGS:
                    if final_cnt[e] > 0:
                        eng.wait_ge(sem[("e", e)], final_cnt[e])
                for k, v in final_d.items():
                    if v > 0:
                        eng.wait_ge(sem[k], v)
                if final_cc > 0:
                    eng.wait_ge(sem[("cc",)], final_cc)
                eng.sem_inc(sem[("ph",)], 1)
            eng.wait_ge(sem[("ph",)], ph)

        with nc.Block() as block:
            @block.sync
            def _(e):
                run("sp", e)

            @block.tensor
            def _(e):
                run("pe", e)

            @block.scalar
            def _(e):
                run("act", e)

            @block.vector
            def _(e):
                run("dve", e)

            @block.gpsimd
            def _(e):
                run("pool", e)
        self._reset()


class NS:
    _uid = [0]

    def __init__(self, nc, tag=""):
        self.nc = nc
        NS._uid[0] += 1
        self.sfx = "_%s%d" % (tag, NS._uid[0])

    def sbuf_tensor(self, name, shape, dt):
        return self.nc.sbuf_tensor(name + self.sfx, shape, dt)

    def psum_tensor(self, name, shape, dt):
        return self.nc.psum_tensor(name + self.sfx, shape, dt)


def _eng(nc, name):
    return {"pe": nc.tensor, "act": nc.scalar, "dve": nc.vector, "pool": nc.gpsimd, "sp": nc.sync}[name]


class Ring:
    def __init__(self, items):
        self.items = items
        self.i = 0

    def next(self):
        it = self.items[self.i % len(self.items)]
        self.i += 1
        return it


def gemm_T(P, wdram, nblocks, KC, xin, xin_tok, ttiles, wslots, psring, epilogue, mrows=None, blk_of=None):
    for bi in range(nblocks):
        wb, wtok = wslots.next()
        src = wdram[blk_of(bi) if blk_of else bi]
        P.dma("pool", lambda e, wb=wb, src=src: e.dma_start(out=wb[:], in_=src), reads=(), writes=(wtok,))
        m = 128 if mrows is None else mrows(bi)
        for ti, (t0, tw) in enumerate(ttiles):
            ps, ptok = psring.next()

            def mm(e, wb=wb, ps=ps, t0=t0, tw=tw, m=m):
                inst = None
                for kc in range(KC):
                    inst = e.matmul(ps[:m, :tw], lhsT=wb[:, kc, :m], rhs=xin[:, kc, t0:t0 + tw],
                                    start=(kc == 0), stop=(kc == KC - 1))
                return inst

            P.op("pe", mm, reads=(wtok, xin_tok), writes=(ptok,))
            epilogue(bi, ti, (t0, tw), ps, ptok)


def build_row(KCo, final):
    nc = bass.Bass("TRN2", target_bir_lowering=False)
    io = dict(
        xT=nc.dram_tensor("xT", [D, TLH], F32, kind="ExternalInput").ap(),
        ynT=nc.dram_tensor("ynT", [KCo * 128, TLH], BF16, kind="ExternalInput").ap(),
        w_o=nc.dram_tensor("w_o", [32, 128, KCo, 128], F32, kind="ExternalInput").ap(),
        nf=nc.dram_tensor("nf", [128, 32], F32, kind="ExternalInput").ap(),
        w_up=nc.dram_tensor("w_up", [2 * NFB, 128, 32, 128], F32, kind="ExternalInput").ap(),
        cw=nc.dram_tensor("cw", [128, 2 * NFB, 3], F32, kind="ExternalInput").ap(),
        cb=nc.dram_tensor("cb", [128, 2 * NFB], F32, kind="ExternalInput").ap(),
        w_dn=[nc.dram_tensor("w_dn%d" % g, [32, 128, GROUPS[g], 128], F32, kind="ExternalInput").ap()
              for g in range(4)],
        fin=nc.dram_tensor("fin", [128, 32], F32, kind="ExternalInput").ap(),
        out=nc.dram_tensor("out", [D, TL], F32, kind="ExternalOutput").ap(),
        xm=nc.dram_tensor("xm", [D, TLH], F32, kind="Internal").ap(),
    )
    with ExitStack() as es0, nc.allow_low_precision("bf16 matmul operands, fp32 accumulation"):
        P = Prog(nc, es0)
        emit_row(nc, P, io, KCo, final)
    return nc


def emit_row(nc_real, P, io, KCo, final, tag="r"):
    nc = NS(nc_real, tag)
    xT, ynT, w_o, nf, w_up, cw, cb, w_dn, fin, out, xm = (io[k] for k in (
        "xT", "ynT", "w_o", "nf", "w_up", "cw", "cb", "w_dn", "fin", "out", "xm"))
    with ExitStack() as es:
        pst = [es.enter_context(nc.psum_tensor("ps%d" % i, [128, 512], F32)) for i in range(8)]
        ones = es.enter_context(nc.sbuf_tensor("ones", [128, 128], BF16))
        nft = es.enter_context(nc.sbuf_tensor("nft", [128, 32], F32))
        fint = es.enter_context(nc.sbuf_tensor("fint", [128, 32], F32))
        cwt = es.enter_context(nc.sbuf_tensor("cwt", [128, 2 * NFB, 3], F32))
        cbt = es.enter_context(nc.sbuf_tensor("cbt", [128, 2 * NFB], F32))
        rstd = es.enter_context(nc.sbuf_tensor("rstd", [128, TLH], F32))

        P.op("dve", lambda e: e.memset(ones[:], 1.0), writes=("ones",))
        P.dma("sp", lambda e: e.dma_start(out=nft[:], in_=nf), writes=("nft",))
        P.dma("sp", lambda e: e.dma_start(out=fint[:], in_=fin), writes=("fint",))
        P.dma("sp", lambda e: e.dma_start(out=cwt[:], in_=cw), writes=("cwt",))
        P.dma("sp", lambda e: e.dma_start(out=cbt[:], in_=cb), writes=("cbt",))

        with ExitStack() as ph:
            yn = ph.enter_context(nc.sbuf_tensor("yn", [128, KCo, TLH], BF16))
            wsl = Ring([(ph.enter_context(nc.sbuf_tensor("wo%d" % i, [128, KCo, 128], BF16)), "wo%d" % i)
                        for i in range(2)])
            xts = Ring([(ph.enter_context(nc.sbuf_tensor("xt%d" % i, [128, TLH], F32)), "xt%d" % i)
                        for i in range(2)])
            sqs = Ring([(ph.enter_context(nc.sbuf_tensor("sq%d" % i, [128, TLH], BF16)), "sq%d" % i)
                        for i in range(2)])
            psr = Ring([(pst[i], "ps%d" % i) for i in range(4)])
            ynv = ynT.rearrange("(kc p) t -> p kc t", p=128)
            half = KCo // 2
            P.dma("sp", lambda e: e.dma_start(out=yn[:, :half, :], in_=ynv[:, :half, :]), writes=("yn",))
            P.dma("sp", lambda e: e.dma_start(out=yn[:, half:, :], in_=ynv[:, half:, :]), writes=("yn",))
            cur = {}

            def epi1(bi, ti, tt, ps, ptok):
                t0, tw = tt
                if ti == 0:
                    xt, xtok = xts.next()
                    sq, sqtok = sqs.next()
                    cur["x"] = (xt, xtok, sq, sqtok)
                    P.dma("sp", lambda e, xt=xt, bi=bi: e.dma_start(out=xt[:], in_=xT[bi * 128:(bi + 1) * 128, :]),
                          writes=(xtok,))
                xt, xtok, sq, sqtok = cur["x"]
                P.op("dve", lambda e: e.tensor_tensor(out=xt[:, t0:t0 + tw], in0=xt[:, t0:t0 + tw],
                                                      in1=ps[:, :tw], op=ALU.add),
                     reads=(ptok, xtok), writes=(xtok,))
                if ti == 2:
                    P.dma("sp", lambda e: e.dma_start(out=xm[bi * 128:(bi + 1) * 128, :], in_=xt[:]),
                          reads=(xtok,), writes=(("xm", bi),))
                    P.op("act", lambda e: e.activation(out=sq[:], in_=xt[:], func=AF.Square),
                         reads=(xtok,), writes=(sqtok,))

                    def ssmm(e):
                        inst = None
                        for j, (a, w) in enumerate(TT3):
                            inst = e.matmul(pst[4 + j][:, :w], lhsT=ones[:], rhs=sq[:, a:a + w],
                                            start=(bi == 0), stop=(bi == 31))
                        return inst

                    P.op("pe", ssmm, reads=(sqtok, "ones"), writes=("ss",))

            gemm_T(P, w_o, 32, KCo, yn, "yn", TT3, wsl, psr, epi1)
            for j, (a, w) in enumerate(TT3):
                P.op("act", lambda e, j=j, a=a, w=w: e.activation(out=rstd[:, a:a + w], in_=pst[4 + j][:, :w],
                                                                  func=AF.Sqrt, bias=EPS, scale=1.0 / D),
                     reads=("ss",), writes=("rstd",))
            P.op("dve", lambda e: e.reciprocal(out=rstd[:], in_=rstd[:]), reads=("rstd",), writes=("rstd",))
            P.flush()

        u2 = es.enter_context(nc.sbuf_tensor("u2", [128, 32, TLH], BF16))
        with ExitStack() as ph:
            xts = Ring([(ph.enter_context(nc.sbuf_tensor("xu%d" % i, [128, TLH], F32)), "xu%d" % i)
                        for i in range(3)])
            for bi in range(32):
                xt, xtok = xts.next()
                P.dma("sp", lambda e, xt=xt, bi=bi: e.dma_start(out=xt[:], in_=xm[bi * 128:(bi + 1) * 128, :]),
                      writes=(xtok,))
                P.op("dve", lambda e, xt=xt, bi=bi: e.scalar_tensor_tensor(
                    out=u2[:, bi, :], in0=xt[:], scalar=nft[:, bi:bi + 1], in1=rstd[:], op0=ALU.mult, op1=ALU.mult),
                    reads=(xtok, "nft", "rstd"), writes=("u2",))
            P.flush()

        with ExitStack() as ph:
            act = ph.enter_context(nc.sbuf_tensor("actb", [128, 22, TL], BF16))
            wsl = Ring([(ph.enter_context(nc.sbuf_tensor("wu%d" % i, [128, 32, 128], BF16)), "wu%d" % i)
                        for i in range(4)])
            wds = Ring([(ph.enter_context(nc.sbuf_tensor("wd%d" % i, [128, 22, 128], BF16)), "wd%d" % i)
                        for i in range(2)])
            hts = Ring([(ph.enter_context(nc.sbuf_tensor("ht%d" % i, [128, TLH], F32)), "ht%d" % i)
                        for i in range(3)])
            cgs = Ring([(ph.enter_context(nc.sbuf_tensor("cg%d" % i, [128, TL], F32)), "cg%d" % i)
                        for i in range(2)])
            cvs = Ring([(ph.enter_context(nc.sbuf_tensor("cv%d" % i, [128, TL], F32)), "cv%d" % i)
                        for i in range(2)])
            xas = Ring([(ph.enter_context(nc.sbuf_tensor("xa%d" % i, [128, TL], F32)), "xa%d" % i)
                        for i in range(2)])
            psr = Ring([(pst[i], "ps%d" % i) for i in range(6)])
            psd = Ring([(pst[6 + i], "ps%d" % (6 + i)) for i in range(2)])
            goff = 0
            for g in range(4):
                CG = GROUPS[g]
                st = {}

                def blk_of(bi, goff=goff):
                    j, isv = bi // 2, bi % 2
                    return (NFB if isv else 0) + goff + j

                def epi3(bi, ti, tt, ps, ptok, goff=goff):
                    t0, tw = tt
                    j, isv = bi // 2, bi % 2
                    fb = blk_of(bi)
                    if ti == 0:
                        st["h"] = hts.next()
                    ht, htok = st["h"]
                    P.op("act", lambda e: e.activation(out=ht[:, t0:t0 + tw], in_=ps[:, :tw], func=AF.Copy),
                         reads=(ptok,), writes=(htok,))
                    if ti != 2:
                        return
                    c, ctok = (cvs if isv else cgs).next()
                    P.op("dve", lambda e: e.tensor_scalar(out=c[:], in0=ht[:, 0:TL], scalar1=cwt[:, fb, 0:1],
                                                          scalar2=cbt[:, fb:fb + 1], op0=ALU.mult, op1=ALU.add),
                         reads=(htok, "cwt", "cbt"), writes=(ctok,))
                    for k in (1, 2):
                        P.op("dve", lambda e, k=k: e.scalar_tensor_tensor(
                            out=c[:], in0=ht[:, k:k + TL], scalar=cwt[:, fb, k:k + 1], in1=c[:],
                            op0=ALU.mult, op1=ALU.add), reads=(htok, ctok, "cwt"), writes=(ctok,))
                    if not isv:
                        st["g"] = (c, ctok)
                        return
                    cg, cgtok = st["g"]
                    P.op("act", lambda e: e.activation(out=ht[:, 0:TL], in_=cg[:], func=AF.Silu),
                         reads=(cgtok,), writes=(htok,))
                    P.op("dve", lambda e: e.tensor_tensor(out=act[:, j, :], in0=ht[:, 0:TL], in1=c[:], op=ALU.mult),
                         reads=(htok, ctok), writes=(("act", j),))

                gemm_T(P, w_up, 2 * CG, 32, u2, "u2", TT3, wsl, psr, epi3, blk_of=blk_of)

                for nb in range(32):
                    wd, wdtok = wds.next()
                    P.dma("pool", lambda e, wd=wd, nb=nb, g=g, CG=CG: e.dma_start(out=wd[:, :CG, :], in_=w_dn[g][nb]),
                          writes=(wdtok,))
                    xa, xatok = xas.next()
                    if g == 0:
                        P.dma("sp", lambda e, xa=xa, nb=nb: e.dma_start(
                            out=xa[:], in_=xm[nb * 128:(nb + 1) * 128, HALO:TLH]),
                            reads=(("xm", nb),), writes=(xatok,))
                    else:
                        P.dma("sp", lambda e, xa=xa, nb=nb: e.dma_start(
                            out=xa[:], in_=out[nb * 128:(nb + 1) * 128, :]),
                            reads=(("out", nb),), writes=(xatok,))
                    for hh in range(2):
                        ps, ptok = psd.next()

                        def mmd(e, wd=wd, ps=ps, hh=hh, CG=CG):
                            inst = None
                            for kc in range(CG):
                                inst = e.matmul(ps[:, :], lhsT=wd[:, kc, :], rhs=act[:, kc, hh * 512:(hh + 1) * 512],
                                                start=(kc == 0), stop=(kc == CG - 1))
                            return inst

                        P.op("pe", mmd, reads=(wdtok,) + tuple(("act", j) for j in range(CG)), writes=(ptok,))
                        P.op("dve", lambda e, xa=xa, ps=ps, hh=hh: e.tensor_tensor(
                            out=xa[:, hh * 512:(hh + 1) * 512], in0=xa[:, hh * 512:(hh + 1) * 512], in1=ps[:, :],
                            op=ALU.add), reads=(ptok, xatok), writes=(xatok,))
                    P.dma("sp", lambda e, xa=xa, nb=nb: e.dma_start(out=out[nb * 128:(nb + 1) * 128, :], in_=xa[:]),
                          reads=(xatok,), writes=(("out", nb),))
                goff += CG
            P.flush()

        if final:
            with ExitStack() as ph:
                xts = Ring([(ph.enter_context(nc.sbuf_tensor("xf%d" % i, [128, TL], F32)), "xf%d" % i)
                            for i in range(3)])
                sqs = Ring([(ph.enter_context(nc.sbuf_tensor("sf%d" % i, [128, TL], BF16)), "sf%d" % i)
                            for i in range(2)])
                for bi in range(32):
                    xt, xtok = xts.next()
                    sq, sqtok = sqs.next()
                    P.dma("sp", lambda e, xt=xt, bi=bi: e.dma_start(out=xt[:], in_=out[bi * 128:(bi + 1) * 128, :]),
                          writes=(xtok,))
                    P.op("act", lambda e, xt=xt, sq=sq: e.activation(out=sq[:], in_=xt[:], func=AF.Square),
                         reads=(xtok,), writes=(sqtok,))

                    def ssmm(e, sq=sq, bi=bi):
                        inst = None
                        for j in range(2):
                            inst = e.matmul(pst[j][:, :], lhsT=ones[:], rhs=sq[:, j * 512:(j + 1) * 512],
                                            start=(bi == 0), stop=(bi == 31))
                        return inst

                    P.op("pe", ssmm, reads=(sqtok, "ones"), writes=("ssf",))
                for j in range(2):
                    P.op("act", lambda e, j=j: e.activation(out=rstd[:, j * 512:(j + 1) * 512], in_=pst[j][:, :],
                                                            func=AF.Sqrt, bias=EPS, scale=1.0 / D),
                         reads=("ssf",), writes=("rstd",))
                P.op("dve", lambda e: e.reciprocal(out=rstd[:, :TL], in_=rstd[:, :TL]), reads=("rstd",),
                     writes=("rstd",))
                for bi in range(32):
                    xt, xtok = xts.next()
                    P.dma("sp", lambda e, xt=xt, bi=bi: e.dma_start(out=xt[:], in_=out[bi * 128:(bi + 1) * 128, :]),
                          writes=(xtok,))
                    P.op("dve", lambda e, xt=xt, bi=bi: e.scalar_tensor_tensor(
                        out=xt[:], in0=xt[:], scalar=fint[:, bi:bi + 1], in1=rstd[:, :TL], op0=ALU.mult,
                        op1=ALU.mult), reads=(xtok, "fint", "rstd"), writes=(xtok,))
                    P.dma("sp", lambda e, xt=xt, bi=bi: e.dma_start(out=out[bi * 128:(bi + 1) * 128, :], in_=xt[:]),
                          reads=(xtok,), writes=(("out", bi),))
                P.flush()


def tile_w(W):
    Kd, N = W.shape
    return np.ascontiguousarray(W.reshape(Kd // 128, 128, N // 128, 128).transpose(2, 1, 0, 3))


def col_pb(v):
    return np.ascontiguousarray(v.reshape(-1, 128).T)


def row_inputs(xT, ynT, w_out, nfw, w_up, cw, cb, w_dn, finw):
    ins = {
        "xT": np.ascontiguousarray(xT, dtype=np.float32),
        "ynT": np.ascontiguousarray(ynT),
        "w_o": tile_w(w_out),
        "nf": col_pb(nfw),
        "w_up": tile_w(w_up),
        "cw": np.ascontiguousarray(cw.T.reshape(2 * NFB, 128, 3).transpose(1, 0, 2)),
        "cb": col_pb(cb),
        "fin": col_pb(finw),
    }
    goff = 0
    for g, CG in enumerate(GROUPS):
        ins["w_dn%d" % g] = tile_w(w_dn[goff * 128:(goff + CG) * 128, :])
        goff += CG
    return ins


def norm_phase(P, nc, xT, nmt, nmtok, uT, ones, pst, T):
    xv = xT.rearrange("(kc p) t -> p kc t", p=128)
    uv = uT.rearrange("(kc p) t -> p kc t", p=128)
    with ExitStack() as ph:
        xins = Ring([(ph.enter_context(nc.sbuf_tensor("nx%d" % i, [128, 32, 512], F32)), "nx%d" % i)
                     for i in range(2)])
        uos = Ring([(ph.enter_context(nc.sbuf_tensor("nu%d" % i, [128, 32, 512], BF16)), "nu%d" % i)
                    for i in range(2)])
        sqs = Ring([(ph.enter_context(nc.sbuf_tensor("nq%d" % i, [128, 512], BF16)), "nq%d" % i)
                    for i in range(3)])
        rss = Ring([(ph.enter_context(nc.sbuf_tensor("nr%d" % i, [128, 512], F32)), "nr%d" % i)
                    for i in range(2)])
        pss = Ring([(pst[i], "ps%d" % i) for i in range(2)])
        for tt in range(T // 512):
            t0 = tt * 512
            xin, xtok = xins.next()
            uo, utok = uos.next()
            rs, rtok = rss.next()
            ps, ptok = pss.next()
            P.dma("sp", lambda e, xin=xin, t0=t0: e.dma_start(out=xin[:], in_=xv[:, :, t0:t0 + 512]),
                  writes=(xtok,))
            for bi in range(32):
                sq, sqtok = sqs.next()
                P.op("act", lambda e, sq=sq, xin=xin, bi=bi: e.activation(out=sq[:], in_=xin[:, bi, :],
                                                                           func=AF.Square),
                     reads=(xtok,), writes=(sqtok,))
                P.op("pe", lambda e, sq=sq, ps=ps, bi=bi: e.matmul(ps[:, :], lhsT=ones[:], rhs=sq[:],
                                                                   start=(bi == 0), stop=(bi == 31)),
                     reads=(sqtok, "ones"), writes=(ptok,))
            P.op("act", lambda e, rs=rs, ps=ps: e.activation(out=rs[:], in_=ps[:, :], func=AF.Sqrt, bias=EPS,
                                                            scale=1.0 / D), reads=(ptok,), writes=(rtok,))
            P.op("dve", lambda e, rs=rs: e.reciprocal(out=rs[:], in_=rs[:]), reads=(rtok,), writes=(rtok,))
            for bi in range(32):
                P.op("dve", lambda e, uo=uo, xin=xin, rs=rs, bi=bi: e.scalar_tensor_tensor(
                    out=uo[:, bi, :], in0=xin[:, bi, :], scalar=nmt[:, bi:bi + 1], in1=rs[:], op0=ALU.mult,
                    op1=ALU.mult), reads=(xtok, rtok, nmtok), writes=(utok,))
            P.dma("sp", lambda e, uo=uo, t0=t0: e.dma_start(out=uv[:, :, t0:t0 + 512], in_=uo[:]),
                  reads=(utok,), writes=(("uT", tt),))
        P.flush()


HPC = 4
ATT_SCALE = 128 ** -0.5


def build_attn(lambda_init):
    T = SEQ
    nc = bass.Bass("TRN2", target_bir_lowering=False)
    io = dict(
        xT=nc.dram_tensor("xT", [D, T], F32, kind="ExternalInput").ap(),
        nm=nc.dram_tensor("nm", [128, 32], F32, kind="ExternalInput").ap(),
        w_qk=nc.dram_tensor("w_qk", [16, 128, 32, 128], F32, kind="ExternalInput").ap(),
        w_v=nc.dram_tensor("w_v", [2, 128, 32, 512], F32, kind="ExternalInput").ap(),
        pos=nc.dram_tensor("pos", [1, T], mybir.dt.int32, kind="ExternalInput").ap(),
        invf=nc.dram_tensor("invf", [128, 1], F32, kind="ExternalInput").ap(),
        rmat=nc.dram_tensor("rmat", [128, 128], F32, kind="ExternalInput").ap(),
        masks=nc.dram_tensor("masks", [128, 4, 512], BF16, kind="ExternalInput").ap(),
        lqk=nc.dram_tensor("lqk", [128, 4], F32, kind="ExternalInput").ap(),
        sub=nc.dram_tensor("sub", [128, 2], F32, kind="ExternalInput").ap(),
        onT=nc.dram_tensor("onT", [HPC * 256, T], BF16, kind="ExternalOutput").ap(),
        uT=nc.dram_tensor("uT", [D, T], BF16, kind="Internal").ap(),
        qk=nc.dram_tensor("qk", [16, 128, T], BF16, kind="Internal").ap(),
        vtm=nc.dram_tensor("vtm", [T, HPC * 256], BF16, kind="Internal").ap(),
    )
    with ExitStack() as es0, nc.allow_low_precision("bf16 matmul operands, fp32 accumulation"):
        P = Prog(nc, es0)
        emit_attn(nc, P, io, lambda_init)
    return nc


def emit_attn(nc_real, P, io, lambda_init, tag="a", do_norm=True):
    T = SEQ
    nc = NS(nc_real, tag)
    xT, nm, w_qk, w_v, pos, invf, rmat, masks, lqk, sub, onT, uT, qk, vtm = (io[k] for k in (
        "xT", "nm", "w_qk", "w_v", "pos", "invf", "rmat", "masks", "lqk", "sub", "onT", "uT", "qk", "vtm"))
    with ExitStack() as es:
        pst = [es.enter_context(nc.psum_tensor("ps%d" % i, [128, 512], F32)) for i in range(8)]
        ones = es.enter_context(nc.sbuf_tensor("ones", [128, 128], BF16))
        onesf = es.enter_context(nc.sbuf_tensor("onesf", [128, 128], F32))
        nmt = es.enter_context(nc.sbuf_tensor("nmt", [128, 32], F32))
        rmt = es.enter_context(nc.sbuf_tensor("rmt", [128, 128], F32))
        ivt = es.enter_context(nc.sbuf_tensor("ivt", [128, 1], F32))
        lqt = es.enter_context(nc.sbuf_tensor("lqt", [128, 4], F32))
        lpt = es.enter_context(nc.sbuf_tensor("lpt", [128, 2], F32))
        nlam = es.enter_context(nc.sbuf_tensor("nlam", [128, 1], F32))
        subt = es.enter_context(nc.sbuf_tensor("subt", [128, 2], F32))
        mkt = es.enter_context(nc.sbuf_tensor("mkt", [128, 4, 512], BF16))
        P.op("dve", lambda e: e.memset(ones[:], 1.0), writes=("ones",))
        P.op("dve", lambda e: e.memset(onesf[:], 1.0), writes=("onesf",))
        for dst, src, tok in ((nmt, nm, "nmt"), (rmt, rmat, "rmt"), (ivt, invf, "ivt"), (lqt, lqk, "lqt"),
                              (subt, sub, "subt"), (mkt, masks, "mkt")):
            P.dma("sp", lambda e, dst=dst, src=src: e.dma_start(out=dst[:], in_=src), writes=(tok,))
        P.op("dve", lambda e: e.tensor_tensor(out=lpt[:, 0:1], in0=lqt[:, 0:1], in1=lqt[:, 1:2], op=ALU.mult),
             reads=("lqt",), writes=("lpt",))
        P.op("dve", lambda e: e.tensor_tensor(out=lpt[:, 1:2], in0=lqt[:, 2:3], in1=lqt[:, 3:4], op=ALU.mult),
             reads=("lqt",), writes=("lpt",))
        P.op("pe", lambda e: e.matmul(pst[7][:, 0:2], lhsT=onesf[:], rhs=lpt[:], start=True, stop=True),
             reads=("lpt", "onesf"), writes=("ps7",))
        P.op("act", lambda e: e.activation(out=lpt[:], in_=pst[7][:, 0:2], func=AF.Exp), reads=("ps7",),
             writes=("lpt",))
        P.op("dve", lambda e: e.tensor_tensor(out=nlam[:], in0=lpt[:, 1:2], in1=lpt[:, 0:1], op=ALU.subtract),
             reads=("lpt",), writes=("nlam",))
        P.op("dve", lambda e: e.tensor_scalar(out=nlam[:], in0=nlam[:], scalar1=-float(lambda_init), scalar2=None,
                                              op0=ALU.add), reads=("nlam",), writes=("nlam",))
        P.op("dve", lambda e: e.tensor_scalar(out=subt[:], in0=subt[:], scalar1=float(1.0 - lambda_init),
                                              scalar2=None, op0=ALU.mult), reads=("subt",), writes=("subt",))

        if do_norm:
            norm_phase(P, nc, xT, nmt, "nmt", uT, ones, pst, T)

        cost = es.enter_context(nc.sbuf_tensor("cost", [128, T], F32))
        sint = es.enter_context(nc.sbuf_tensor("sint", [128, T], F32))
        with ExitStack() as ph:
            posi = ph.enter_context(nc.sbuf_tensor("posi", [128, T], mybir.dt.int32))
            ang = ph.enter_context(nc.sbuf_tensor("ang", [128, T], F32))
            tmp = ph.enter_context(nc.sbuf_tensor("angt", [128, T], F32))
            P.dma("sp", lambda e: e.dma_start(out=posi[:], in_=pos.partition_broadcast(128)), writes=("posi",))
            P.op("dve", lambda e: e.tensor_copy(out=ang[:], in_=posi[:]), reads=("posi",), writes=("ang",))
            P.op("dve", lambda e: e.tensor_scalar(out=ang[:], in0=ang[:], scalar1=ivt[:, 0:1], scalar2=None,
                                                  op0=ALU.mult), reads=("ang", "ivt"), writes=("ang",))
            ki = ph.enter_context(nc.sbuf_tensor("angk", [128, T], mybir.dt.int32))
            TWO_PI = 2.0 * math.pi
            for dst, shift, tok in ((sint, 0.0, "sint"), (cost, 0.5 * math.pi, "cost")):
                P.op("dve", lambda e, shift=shift: e.tensor_scalar(out=tmp[:], in0=ang[:], scalar1=shift,
                                                                   scalar2=None, op0=ALU.add),
                     reads=("ang",), writes=("angt",))
                P.op("dve", lambda e: e.tensor_scalar(out=ki[:], in0=tmp[:], scalar1=1.0 / TWO_PI, scalar2=None,
                                                      op0=ALU.mult), reads=("angt",), writes=("angk",))
                P.op("dve", lambda e, dst=dst: e.tensor_copy(out=dst[:], in_=ki[:]), reads=("angk",), writes=(tok,))
                P.op("dve", lambda e, dst=dst: e.scalar_tensor_tensor(out=tmp[:], in0=dst[:], scalar=-TWO_PI,
                                                                      in1=tmp[:], op0=ALU.mult, op1=ALU.add),
                     reads=(tok, "angt"), writes=("angt",))
                P.op("dve", lambda e, dst=dst: e.tensor_scalar(out=dst[:], in0=tmp[:], scalar1=math.pi,
                                                               scalar2=TWO_PI, op0=ALU.is_gt, op1=ALU.mult),
                     reads=("angt",), writes=(tok,))
                P.op("dve", lambda e, dst=dst: e.tensor_tensor(out=tmp[:], in0=tmp[:], in1=dst[:], op=ALU.subtract),
                     reads=("angt", tok), writes=("angt",))
                P.op("act", lambda e, dst=dst: e.activation(out=dst[:], in_=tmp[:], func=AF.Sin),
                     reads=("angt",), writes=(tok,))
            P.flush()

        with ExitStack() as ph:
            us = Ring([(ph.enter_context(nc.sbuf_tensor("ut%d" % i, [128, 32, 1024], BF16)), "ut%d" % i)
                       for i in range(1)])
            wsl = Ring([(ph.enter_context(nc.sbuf_tensor("wq%d" % i, [128, 32, 128], BF16)), "wq%d" % i)
                        for i in range(2)])
            wvs = Ring([(ph.enter_context(nc.sbuf_tensor("wv%d" % i, [128, 32, 512], BF16)), "wv%d" % i)
                        for i in range(1)])
            qfs = Ring([(ph.enter_context(nc.sbuf_tensor("qf%d" % i, [128, 512], F32)), "qf%d" % i)
                        for i in range(3)])
            t1s = Ring([(ph.enter_context(nc.sbuf_tensor("t1%d" % i, [128, 512], F32)), "t1%d" % i)
                        for i in range(2)])
            qos = Ring([(ph.enter_context(nc.sbuf_tensor("qo%d" % i, [128, 512], BF16)), "qo%d" % i)
                        for i in range(3)])
            vos = Ring([(ph.enter_context(nc.sbuf_tensor("vo%d" % i, [128, 512], BF16)), "vo%d" % i)
                        for i in range(3)])
            psr = Ring([(pst[i], "ps%d" % i) for i in range(4)])
            psrot = Ring([(pst[4 + i], "ps%d" % (4 + i)) for i in range(2)])
            psv = Ring([(pst[6 + i], "ps%d" % (6 + i)) for i in range(2)])
            uv = uT.rearrange("(kc p) t -> p kc t", p=128)
            for t4 in range(T // 1024):
                tb = t4 * 1024
                ut, utok = us.next()
                for hlf in range(2):
                    P.dma("sp", lambda e, ut=ut, tb=tb, hlf=hlf: e.dma_start(
                        out=ut[:, hlf * 16:(hlf + 1) * 16, :], in_=uv[:, hlf * 16:(hlf + 1) * 16, tb:tb + 1024]),
                        writes=(utok,))

                def epi2(bi, ti, tt, ps, ptok, tb=tb):
                    t0, tw = tt
                    g0 = tb + t0
                    qf, qftok = qfs.next()
                    t1, t1tok = t1s.next()
                    qo, qotok = qos.next()
                    pr, prtok = psrot.next()
                    P.op("act", lambda e: e.activation(out=qf[:], in_=ps[:, :], func=AF.Copy), reads=(ptok,),
                         writes=(qftok,))
                    P.op("pe", lambda e: e.matmul(pr[:, :], lhsT=rmt[:], rhs=qf[:], start=True, stop=True),
                         reads=(qftok, "rmt"), writes=(prtok,))
                    P.op("dve", lambda e: e.tensor_tensor(out=t1[:], in0=pr[:, :], in1=sint[:, g0:g0 + 512],
                                                          op=ALU.mult), reads=(prtok, "sint"), writes=(t1tok,))
                    P.op("pool", lambda e: e.tensor_tensor(out=qf[:], in0=qf[:], in1=cost[:, g0:g0 + 512],
                                                           op=ALU.mult), reads=(qftok, "cost"), writes=(qftok,))
                    P.op("dve", lambda e: e.tensor_tensor(out=qo[:], in0=qf[:], in1=t1[:], op=ALU.add),
                         reads=(qftok, t1tok), writes=(qotok,))
                    P.dma("sp", lambda e: e.dma_start(out=qk[bi, :, g0:g0 + 512], in_=qo[:]), reads=(qotok,),
                          writes=(("qk", bi, g0),))

                gemm_T(P, w_qk, 16, 32, ut, utok, [(0, 512), (512, 512)], wsl, psr, epi2)
                for ct in range(2):
                    wv, wvtok = wvs.next()
                    P.dma("pool", lambda e, wv=wv, ct=ct: e.dma_start(out=wv[:], in_=w_v[ct]), writes=(wvtok,))
                    for ts in range(8):
                        ps, ptok = psv.next()
                        vo, votok = vos.next()

                        def mmv(e, ps=ps, wv=wv, ut=ut, ts=ts):
                            inst = None
                            for kc in range(32):
                                inst = e.matmul(ps[:, :], lhsT=ut[:, kc, ts * 128:(ts + 1) * 128], rhs=wv[:, kc, :],
                                                start=(kc == 0), stop=(kc == 31))
                            return inst

                        P.op("pe", mmv, reads=(wvtok, utok), writes=(ptok,))
                        P.op("act", lambda e, vo=vo, ps=ps: e.activation(out=vo[:], in_=ps[:, :], func=AF.Copy),
                             reads=(ptok,), writes=(votok,))
                        r0 = tb + ts * 128
                        P.dma("sp", lambda e, vo=vo, r0=r0, ct=ct: e.dma_start(
                            out=vtm[r0:r0 + 128, ct * 512:(ct + 1) * 512], in_=vo[:]), reads=(votok,),
                            writes=(("vtm", r0, ct),))
            P.flush()

        with ExitStack() as ph:
            qts = Ring([(ph.enter_context(nc.sbuf_tensor("qT%d" % i, [128, 2, T], BF16)), "qT%d" % i)
                        for i in range(2)])
            kts = Ring([(ph.enter_context(nc.sbuf_tensor("kT%d" % i, [128, 2, T], BF16)), "kT%d" % i)
                        for i in range(2)])
            vts = Ring([(ph.enter_context(nc.sbuf_tensor("vT%d" % i, [128, 32, 256], BF16)), "vT%d" % i)
                        for i in range(2)])
            pts = Ring([(ph.enter_context(nc.sbuf_tensor("pT%d" % i, [128, 512], BF16)), "pT%d" % i)
                        for i in range(4)])
            rd = ph.enter_context(nc.sbuf_tensor("rd", [128, 512], F32))
            ot = ph.enter_context(nc.sbuf_tensor("ot", [128, 2, 512], F32))
            o1 = ph.enter_context(nc.sbuf_tensor("o1", [128, 2, 512], F32))
            sqo = ph.enter_context(nc.sbuf_tensor("sqo", [128, 2, 512], BF16))
            rso = ph.enter_context(nc.sbuf_tensor("rso", [128, 512], F32))
            ons = Ring([(ph.enter_context(nc.sbuf_tensor("on%d" % i, [128, 2, 512], BF16)), "on%d" % i)
                        for i in range(2)])
            pss = Ring([(pst[i], "ps%d" % i) for i in range(3)])
            pso = [pst[3], pst[4]]
            psden = pst[5]
            psss = pst[6]
            vv = vtm.rearrange("(kb p) v -> p kb v", p=128)
            for hl in range(HPC):
                qT, qtok = qts.next()
                kT, ktok = kts.next()
                vT, vtok = vts.next()
                for c in range(2):
                    P.dma("sp", lambda e, qT=qT, hl=hl, c=c: e.dma_start(out=qT[:, c, :], in_=qk[hl * 4 + c]),
                          writes=(qtok,))
                    P.dma("sp", lambda e, kT=kT, hl=hl, c=c: e.dma_start(out=kT[:, c, :], in_=qk[hl * 4 + 2 + c]),
                          writes=(ktok,))
                P.dma("sp", lambda e, vT=vT, hl=hl: e.dma_start(out=vT[:], in_=vv[:, :, hl * 256:(hl + 1) * 256]),
                      writes=(vtok,))
                for qt in range(T // 512):
                    q0 = qt * 512
                    nkb = (qt + 1) * 4
                    for c in range(2):
                        def s_op(kb, c=c, q0=q0, qT=qT, kT=kT, qtok=qtok, ktok=ktok):
                            ps, ptok = pss.next()
                            P.op("pe", lambda e: e.matmul(ps[:, :], lhsT=kT[:, c, kb * 128:(kb + 1) * 128],
                                                          rhs=qT[:, c, q0:q0 + 512], start=True, stop=True),
                                 reads=(qtok, ktok), writes=(ptok,))
                            return ps, ptok

                        nxt = s_op(0)
                        for kb in range(nkb):
                            ps, ptok = nxt
                            if kb + 1 < nkb:
                                nxt = s_op(kb + 1)
                            pT, pttok = pts.next()
                            P.op("act", lambda e, pT=pT, ps=ps: e.activation(out=pT[:], in_=ps[:, :], func=AF.Exp,
                                                                             scale=ATT_SCALE),
                                 reads=(ptok,), writes=(pttok,))
                            dj = kb - qt * 4
                            if dj >= 0:
                                P.op("pool", lambda e, pT=pT, dj=dj: e.tensor_tensor(
                                    out=pT[:], in0=pT[:], in1=mkt[:, dj, :], op=ALU.mult),
                                    reads=(pttok, "mkt"), writes=(pttok,))

                            def pv(e, pT=pT, kb=kb, vT=vT, nkb=nkb):
                                e.matmul(pso[0][:, :], lhsT=vT[:, kb, 0:128], rhs=pT[:], start=(kb == 0),
                                         stop=(kb == nkb - 1))
                                e.matmul(pso[1][:, :], lhsT=vT[:, kb, 128:256], rhs=pT[:], start=(kb == 0),
                                         stop=(kb == nkb - 1))
                                return e.matmul(psden[:, :], lhsT=ones[:], rhs=pT[:], start=(kb == 0),
                                                stop=(kb == nkb - 1))

                            P.op("pe", pv, reads=(pttok, vtok, "ones"), writes=("pso",))
                        P.op("dve", lambda e: e.reciprocal(out=rd[:], in_=psden[:, :]), reads=("pso",),
                             writes=("rd",))
                        for v in range(2):
                            if c == 0:
                                P.op("dve", lambda e, v=v: e.tensor_tensor(out=o1[:, v, :], in0=pso[v][:, :],
                                                                           in1=rd[:], op=ALU.mult),
                                     reads=("pso", "rd"), writes=("o1",))
                            else:
                                P.op("dve", lambda e, v=v: e.tensor_tensor(out=ot[:, v, :], in0=pso[v][:, :],
                                                                           in1=rd[:], op=ALU.mult),
                                     reads=("pso", "rd"), writes=("ot",))
                                P.op("dve", lambda e, v=v: e.scalar_tensor_tensor(
                                    out=ot[:, v, :], in0=ot[:, v, :], scalar=nlam[:, 0:1], in1=o1[:, v, :],
                                    op0=ALU.mult, op1=ALU.add), reads=("ot", "o1", "nlam"), writes=("ot",))
                    on, ontok = ons.next()
                    P.op("act", lambda e: e.activation(out=sqo[:], in_=ot[:], func=AF.Square), reads=("ot",),
                         writes=("sqo",))

                    def ssm(e):
                        e.matmul(psss[:, :], lhsT=ones[:], rhs=sqo[:, 0, :], start=True, stop=False)
                        return e.matmul(psss[:, :], lhsT=ones[:], rhs=sqo[:, 1, :], start=False, stop=True)

                    P.op("pe", ssm, reads=("sqo", "ones"), writes=("psss",))
                    P.op("act", lambda e: e.activation(out=rso[:], in_=psss[:, :], func=AF.Sqrt, bias=EPS,
                                                       scale=1.0 / 256.0), reads=("psss",), writes=("rso",))
                    P.op("dve", lambda e: e.reciprocal(out=rso[:], in_=rso[:]), reads=("rso",), writes=("rso",))
                    for v in range(2):
                        P.op("dve", lambda e, v=v, on=on: e.scalar_tensor_tensor(
                            out=on[:, v, :], in0=ot[:, v, :], scalar=subt[:, v:v + 1], in1=rso[:], op0=ALU.mult,
                            op1=ALU.mult), reads=("ot", "rso", "subt"), writes=(ontok,))
                        r0 = (hl * 2 + v) * 128
                        P.dma("sp", lambda e, v=v, on=on, r0=r0, q0=q0: e.dma_start(
                            out=onT[r0:r0 + 128, q0:q0 + 512], in_=on[:, v, :]), reads=(ontok,),
                            writes=(("onT", r0, q0),))
            P.flush()


def rope_consts():
    inv = (1.0 / (10000.0 ** (np.arange(0, 128, 2, dtype=np.float32) / np.float32(128)))).astype(np.float32)
    invf = np.concatenate([inv, inv]).reshape(128, 1).astype(np.float32)
    rm = np.zeros((128, 128), np.float32)
    for dp in range(64):
        rm[dp + 64, dp] = -1.0
    for dp in range(64, 128):
        rm[dp - 64, dp] = 1.0
    k = np.arange(128)[:, None, None]
    j = np.arange(4)[None, :, None]
    q = np.arange(512)[None, None, :]
    masks = ((j * 128 + k) <= q).astype(np.float32).astype(NPBF)
    return invf, rm, masks


def attn_inputs(xT_b, positions_b, r, nmw, w_qkv, lq1, lk1, lq2, lk2, subln):
    invf, rm, masks = rope_consts()
    cols = []
    for hl in range(HPC):
        h = r * HPC + hl
        for kind in range(2):
            for c in range(2):
                c0 = kind * D + h * 256 + c * 128
                cols.append(np.arange(c0, c0 + 128))
    cols = np.concatenate(cols)
    wqk = tile_w(w_qkv[:, cols])
    v0 = 2 * D + r * HPC * 256
    wv = w_qkv[:, v0:v0 + HPC * 256]
    wvt = np.ascontiguousarray(wv.reshape(32, 128, 2, 512).transpose(2, 1, 0, 3))
    return {
        "xT": xT_b, "nm": col_pb(nmw), "w_qk": wqk, "w_v": wvt,
        "pos": np.ascontiguousarray(positions_b.reshape(1, -1).astype(np.int32)),
        "invf": invf, "rmat": rm, "masks": masks,
        "lqk": np.ascontiguousarray(np.stack([lq1, lk1, lq2, lk2], axis=1).astype(np.float32)),
        "sub": col_pb(subln),
    }


HM = 32
NBLK_IN = 37


def bc_last(ap, n):
    return ap.unsqueeze(2).to_broadcast([ap.shape[0], ap.shape[1], n])


def bc_mid(ap, n):
    return ap.unsqueeze(1).to_broadcast([ap.shape[0], n, ap.shape[1]])


def build_mamba():
    T = SEQ
    nc = bass.Bass("TRN2", target_bir_lowering=False)
    io = dict(
        xT=nc.dram_tensor("xT", [D, T], F32, kind="ExternalInput").ap(),
        nm=nc.dram_tensor("nm", [128, 32], F32, kind="ExternalInput").ap(),
        w_in=nc.dram_tensor("w_in", [NBLK_IN, 128, 32, 128], F32, kind="ExternalInput").ap(),
        cwm=nc.dram_tensor("cwm", [128, 20, 4], F32, kind="ExternalInput").ap(),
        cbm=nc.dram_tensor("cbm", [128, 20], F32, kind="ExternalInput").ap(),
        dtb=nc.dram_tensor("dtb", [64, 1], F32, kind="ExternalInput").ap(),
        alc=nc.dram_tensor("alc", [64, 1], F32, kind="ExternalInput").ap(),
        sgn=nc.dram_tensor("sgn", [64, 1], F32, kind="ExternalInput").ap(),
        dcl=nc.dram_tensor("dcl", [128, 16], F32, kind="ExternalInput").ap(),
        nwm=nc.dram_tensor("nwm", [128, 16], F32, kind="ExternalInput").ap(),
        ynT=nc.dram_tensor("ynT", [2048, T], BF16, kind="ExternalOutput").ap(),
        uT=nc.dram_tensor("uT", [D, T], BF16, kind="Internal").ap(),
        szT=nc.dram_tensor("szT", [2048, T], F32, kind="Internal").ap(),
        xbcT=nc.dram_tensor("xbcT", [2560, T], F32, kind="Internal").ap(),
        xcT=nc.dram_tensor("xcT", [2560, T], BF16, kind="Internal").ap(),
        dtaT=nc.dram_tensor("dtaT", [64, T], F32, kind="Internal").ap(),
    )
    with ExitStack() as es0, nc.allow_low_precision("bf16 matmul operands, fp32 accumulation"):
        P = Prog(nc, es0)
        emit_mamba(nc, P, io)
    return nc


def emit_mamba(nc_real, P, io, tag="m", do_norm=True):
    T = SEQ
    nc = NS(nc_real, tag)
    xT, nm, w_in, cwm, cbm, dtb, alc, sgn, dcl, nwm, ynT, uT, szT, xbcT, xcT, dtaT = (io[k] for k in (
        "xT", "nm", "w_in", "cwm", "cbm", "dtb", "alc", "sgn", "dcl", "nwm", "ynT", "uT", "szT", "xbcT", "xcT",
        "dtaT"))
    with ExitStack() as es:
        pst = [es.enter_context(nc.psum_tensor("ps%d" % i, [128, 512], F32)) for i in range(7)]
        psT = es.enter_context(nc.psum_tensor("psT", [128, 1024], BF16))
        ones = es.enter_context(nc.sbuf_tensor("ones", [128, 128], BF16))
        onesf = es.enter_context(nc.sbuf_tensor("onesf", [128, 128], F32))
        nmt = es.enter_context(nc.sbuf_tensor("nmt", [128, 32], F32))
        cwt = es.enter_context(nc.sbuf_tensor("cwt", [128, 20, 4], F32))
        cbt = es.enter_context(nc.sbuf_tensor("cbt", [128, 20], F32))
        dtbt = es.enter_context(nc.sbuf_tensor("dtbt", [64, 1], F32))
        amul = es.enter_context(nc.sbuf_tensor("amul", [64, 1], F32))
        sgnt = es.enter_context(nc.sbuf_tensor("sgnt", [64, 1], F32))
        dct = es.enter_context(nc.sbuf_tensor("dct", [128, 16], F32))
        nwt = es.enter_context(nc.sbuf_tensor("nwt", [128, 16], F32))
        P.op("dve", lambda e: e.memset(ones[:], 1.0), writes=("ones",))
        P.op("dve", lambda e: e.memset(onesf[:], 1.0), writes=("onesf",))
        for dst, src, tok in ((nmt, nm, "nmt"), (cwt, cwm, "cwt"), (cbt, cbm, "cbt"), (dtbt, dtb, "dtbt"),
                              (amul, alc, "amul"), (sgnt, sgn, "sgnt"), (dct, dcl, "dct"), (nwt, nwm, "nwt")):
            P.dma("sp", lambda e, dst=dst, src=src: e.dma_start(out=dst[:], in_=src), writes=(tok,))
        P.op("act", lambda e: e.activation(out=amul[:], in_=amul[:], func=AF.Exp), reads=("amul",), writes=("amul",))
        P.op("dve", lambda e: e.tensor_tensor(out=amul[:], in0=amul[:], in1=sgnt[:], op=ALU.mult),
             reads=("amul", "sgnt"), writes=("amul",))

        if do_norm:
            norm_phase(P, nc, xT, nmt, "nmt", uT, ones, pst, T)

        with ExitStack() as ph:
            us = Ring([(ph.enter_context(nc.sbuf_tensor("ut%d" % i, [128, 32, 1024], BF16)), "ut%d" % i)
                       for i in range(1)])
            wsl = Ring([(ph.enter_context(nc.sbuf_tensor("wi%d" % i, [128, 32, 128], BF16)), "wi%d" % i)
                        for i in range(3)])
            evs = Ring([(ph.enter_context(nc.sbuf_tensor("ev%d" % i, [128, 512], F32)), "ev%d" % i)
                        for i in range(4)])
            psr = Ring([(pst[i], "ps%d" % i) for i in range(6)])
            uv = uT.rearrange("(kc p) t -> p kc t", p=128)
            for t4 in range(T // 1024):
                tb = t4 * 1024
                ut, utok = us.next()
                for hlf in range(2):
                    P.dma("sp", lambda e, ut=ut, tb=tb, hlf=hlf: e.dma_start(
                        out=ut[:, hlf * 16:(hlf + 1) * 16, :], in_=uv[:, hlf * 16:(hlf + 1) * 16, tb:tb + 1024]),
                        writes=(utok,))

                def epi(bi, ti, tt, ps, ptok, tb=tb):
                    t0, tw = tt
                    g0 = tb + t0
                    ev, evtok = evs.next()
                    if bi < 16:
                        P.op("act", lambda e: e.activation(out=ev[:], in_=ps[:, :], func=AF.Silu), reads=(ptok,),
                             writes=(evtok,))
                        P.dma("sp", lambda e: e.dma_start(out=szT[bi * 128:(bi + 1) * 128, g0:g0 + 512], in_=ev[:]),
                              reads=(evtok,), writes=(("szT", bi, g0),))
                    elif bi < 36:
                        j = bi - 16
                        P.op("act", lambda e: e.activation(out=ev[:], in_=ps[:, :], func=AF.Copy), reads=(ptok,),
                             writes=(evtok,))
                        P.dma("sp", lambda e: e.dma_start(out=xbcT[j * 128:(j + 1) * 128, g0:g0 + 512], in_=ev[:]),
                              reads=(evtok,), writes=(("xbcT", j, g0),))
                    else:
                        P.op("act", lambda e: e.activation(out=ev[0:64, :], in_=ps[0:64, :], func=AF.Exp,
                                                           bias=dtbt[:, 0:1]), reads=(ptok, "dtbt"), writes=(evtok,))
                        P.op("act", lambda e: e.activation(out=ev[0:64, :], in_=ev[0:64, :], func=AF.Ln, bias=1.0),
                             reads=(evtok,), writes=(evtok,))
                        P.op("dve", lambda e: e.tensor_scalar(out=ev[0:64, :], in0=ev[0:64, :], scalar1=amul[:, 0:1],
                                                              scalar2=None, op0=ALU.mult),
                             reads=(evtok, "amul"), writes=(evtok,))
                        P.dma("sp", lambda e: e.dma_start(out=dtaT[:, g0:g0 + 512], in_=ev[0:64, :]),
                              reads=(evtok,), writes=(("dtaT", g0),))

                gemm_T(P, w_in, NBLK_IN, 32, ut, utok, [(0, 512), (512, 512)], wsl, psr, epi,
                       mrows=lambda bi: 64 if bi == 36 else 128)
            P.flush()

        with ExitStack() as ph:
            cis = Ring([(ph.enter_context(nc.sbuf_tensor("ci%d" % i, [128, T + 3], F32)), "ci%d" % i)
                        for i in range(2)])
            accs = Ring([(ph.enter_context(nc.sbuf_tensor("ca%d" % i, [128, T], F32)), "ca%d" % i)
                         for i in range(2)])
            cos_ = Ring([(ph.enter_context(nc.sbuf_tensor("co%d" % i, [128, T], BF16)), "co%d" % i)
                         for i in range(2)])
            ptmp = ph.enter_context(nc.sbuf_tensor("ptmp", [128, T], F32))
            for ci, citok in cis.items:
                P.op("dve", lambda e, ci=ci: e.memset(ci[:, 0:3], 0.0), writes=(citok + "h",))
            for j in range(20):
                ci, citok = cis.next()
                acc, atok = accs.next()
                co, cotok = cos_.next()
                veng = "dve" if j % 2 == 0 else "pool"
                P.dma("sp", lambda e, ci=ci, j=j: e.dma_start(out=ci[:, 3:T + 3], in_=xbcT[j * 128:(j + 1) * 128, :]),
                      writes=(citok,))
                P.op(veng, lambda e, ci=ci, acc=acc, j=j: e.tensor_scalar(
                    out=acc[:], in0=ci[:, 0:T], scalar1=cwt[:, j, 0:1], scalar2=cbt[:, j:j + 1], op0=ALU.mult,
                    op1=ALU.add), reads=(citok, citok + "h", "cwt", "cbt"), writes=(atok,))
                for k in (1, 2, 3):
                    if veng == "dve":
                        P.op(veng, lambda e, ci=ci, acc=acc, j=j, k=k: e.scalar_tensor_tensor(
                            out=acc[:], in0=ci[:, k:k + T], scalar=cwt[:, j, k:k + 1], in1=acc[:], op0=ALU.mult,
                            op1=ALU.add), reads=(citok, citok + "h", atok, "cwt"), writes=(atok,))
                    else:
                        P.op(veng, lambda e, ci=ci, j=j, k=k: e.tensor_scalar(
                            out=ptmp[:], in0=ci[:, k:k + T], scalar1=cwt[:, j, k:k + 1], scalar2=None,
                            op0=ALU.mult), reads=(citok, citok + "h", "cwt"), writes=("ptmp",))
                        P.op(veng, lambda e, acc=acc: e.tensor_tensor(out=acc[:], in0=acc[:], in1=ptmp[:],
                                                                      op=ALU.add),
                             reads=(atok, "ptmp"), writes=(atok,))
                P.op("act", lambda e, acc=acc, co=co: e.activation(out=co[:], in_=acc[:], func=AF.Silu),
                     reads=(atok,), writes=(cotok,))
                P.dma("sp", lambda e, co=co, j=j: e.dma_start(out=xcT[j * 128:(j + 1) * 128, :], in_=co[:]),
                      reads=(cotok,), writes=(("xcT", j),))
            P.flush()

        with ExitStack() as ph:
            sb = lambda name, shape, dt: ph.enter_context(nc.sbuf_tensor(name, shape, dt))
            trif = sb("trif", [128, 128], F32)
            ntri = sb("ntri", [128, 128], F32)
            idf = sb("idf", [128, 128], F32)
            idb = sb("idb", [128, 128], BF16)
            nmask = sb("nmask", [128, 4, 128], F32)
            S = sb("S", [128, HM * 64], F32)
            Stmp = sb("Stmp", [128, 1024], F32)
            Sbf = sb("Sbf", [128, HM * 64], BF16)
            elast = sb("elast", [128, HM], F32)
            xcss = Ring([(sb("xcs%d" % i, [128, 20, 512], BF16), "xcs%d" % i) for i in range(2)])
            dtas = Ring([(sb("dta%d" % i, [64, 512], F32), "dta%d" % i) for i in range(2)])
            ysp = sb("ysp", [128, 16, 512], F32)
            szs = sb("szs", [128, 16, 512], F32)
            yno = sb("yno", [128, 16, 512], BF16)
            xtm = sb("xtm", [128, HM, 64], BF16)
            xdt = sb("xdt", [128, HM, 64], BF16)
            xd2 = sb("xd2", [128, HM, 64], BF16)
            btm = sb("btm", [128, 2, 128], BF16)
            dtm = sb("dtm", [128, 64], F32)
            gt = sb("gt", [128, 2, 128], F32)
            rh1 = Ring([(sb("rh1%d" % i, [128, 4, 128], F32), "rh1%d" % i) for i in range(2)])
            abc = Ring([(sb("abc%d" % i, [128, 4, 128], F32), "abc%d" % i) for i in range(2)])
            ets = Ring([(sb("et%d" % i, [128, 4, 128], F32), "et%d" % i) for i in range(2)])
            lts = Ring([(sb("lt%d" % i, [128, 4, 128], BF16), "lt%d" % i) for i in range(2)])
            mts = Ring([(sb("mt%d" % i, [128, 4, 128], BF16), "mt%d" % i) for i in range(2)])
            cds = Ring([(sb("cd%d" % i, [128, 4, 128], BF16), "cd%d" % i) for i in range(2)])
            sqs = Ring([(sb("gsq%d" % i, [128, 512], BF16), "gsq%d" % i) for i in range(2)])
            rsg = sb("rsg", [128, 2, 512], F32)
            psM, psC, psL = pst[0], pst[1], pst[2]
            psYs = Ring([(pst[3], "ps3"), (pst[4], "ps4")])
            psS = [pst[5], pst[6]]

            P.op("pool", lambda e: e.memset(trif[:], 1.0), writes=("trif",))
            P.op("pool", lambda e: e.affine_select(out=trif[:], in_=trif[:], pattern=[[1, 128]], compare_op=ALU.is_ge,
                                                   fill=0.0, base=0, channel_multiplier=-1),
                 reads=("trif",), writes=("trif",))
            P.op("pool", lambda e: e.memset(ntri[:], -1.0), writes=("ntri",))
            P.op("pool", lambda e: e.affine_select(out=ntri[:], in_=ntri[:], pattern=[[1, 128]], compare_op=ALU.is_ge,
                                                   fill=0.0, base=0, channel_multiplier=-1),
                 reads=("ntri",), writes=("ntri",))
            P.op("pool", lambda e: e.memset(idf[:], 1.0), writes=("idf",))
            P.op("pool", lambda e: e.affine_select(out=idf[:], in_=idf[:], pattern=[[-1, 128]],
                                                   compare_op=ALU.is_equal, fill=0.0, base=0, channel_multiplier=1),
                 reads=("idf",), writes=("idf",))
            P.op("dve", lambda e: e.tensor_copy(out=idb[:], in_=idf[:]), reads=("idf",), writes=("idb",))
            P.op("pool", lambda e: e.memset(nmask[:], -1e30), writes=("nmask",))
            P.op("pool", lambda e: e.affine_select(out=nmask[:], in_=nmask[:], pattern=[[0, 4], [-1, 128]],
                                                   compare_op=ALU.is_gt, fill=0.0, base=0, channel_multiplier=1),
                 reads=("nmask",), writes=("nmask",))
            P.op("dve", lambda e: e.memset(S[:], 0.0), writes=("S",))
            P.op("dve", lambda e: e.memset(Sbf[:], 0.0), writes=("Sbf",))

            xcv = xcT.rearrange("(j p) t -> p j t", p=128)
            szv = szT.rearrange("(j p) t -> p j t", p=128)
            ynv = ynT.rearrange("(j p) t -> p j t", p=128)
            for sp_i in range(T // 512):
                s0 = sp_i * 512
                xcs, xcstok = xcss.next()
                dta, dtatok = dtas.next()
                P.dma("sp", lambda e, xcs=xcs, s0=s0: e.dma_start(out=xcs[:], in_=xcv[:, :, s0:s0 + 512]),
                      writes=(xcstok,))
                P.dma("sp", lambda e, dta=dta, s0=s0: e.dma_start(out=dta[:], in_=dtaT[:, s0:s0 + 512]),
                      writes=(dtatok,))
                for cc in range(4):
                    c0 = cc * 128
                    for half in range(2):
                        def trx(e, half=half, xcs=xcs, c0=c0):
                            inst = None
                            for jj in range(8):
                                inst = e.transpose(psT[:, jj * 128:(jj + 1) * 128], xcs[:, half * 8 + jj, c0:c0 + 128],
                                                   idb[:])
                            return inst
                        P.op("pe", trx, reads=(xcstok, "idb"), writes=("psT",))
                        P.op("act", lambda e, half=half: e.activation(
                            out=xtm[:, half * 16:(half + 1) * 16, :].rearrange("p h q -> p (h q)"), in_=psT[:, :],
                            func=AF.Copy), reads=("psT",), writes=("xtm",))

                    def trb(e, xcs=xcs, c0=c0):
                        e.transpose(psT[:, 0:128], xcs[:, 16, c0:c0 + 128], idb[:])
                        return e.transpose(psT[:, 128:256], xcs[:, 17, c0:c0 + 128], idb[:])
                    P.op("pe", trb, reads=(xcstok, "idb"), writes=("psT",))
                    P.op("act", lambda e: e.activation(out=btm[:].rearrange("p g n -> p (g n)"), in_=psT[:, 0:256],
                                                       func=AF.Copy), reads=("psT",), writes=("btm",))
                    P.op("pe", lambda e, dta=dta, c0=c0: e.transpose(psM[:, 0:64], dta[0:64, c0:c0 + 128],
                                                                     idf[0:64, 0:64]),
                         reads=(dtatok, "idf"), writes=("psM",))
                    P.op("act", lambda e: e.activation(out=dtm[:], in_=psM[:, 0:64], func=AF.Copy), reads=("psM",),
                         writes=("dtm",))
                    def gmm(e, xcs=xcs, c0=c0):
                        e.matmul(psM[:, 128:256], lhsT=xcs[:, 16, c0:c0 + 128], rhs=xcs[:, 18, c0:c0 + 128],
                                 start=True, stop=True)
                        return e.matmul(psM[:, 256:384], lhsT=xcs[:, 17, c0:c0 + 128], rhs=xcs[:, 19, c0:c0 + 128],
                                        start=True, stop=True)
                    P.op("pe", gmm, reads=(xcstok, "dtm"), writes=("psM",))
                    P.op("act", lambda e: e.activation(out=gt[:].rearrange("p g n -> p (g n)"), in_=psM[:, 128:384],
                                                       func=AF.Copy), reads=("psM",), writes=("gt",))
                    P.op("dve", lambda e: e.tensor_tensor(out=xdt[:], in0=xtm[:], in1=bc_last(dtm[:, 0:32], 64),
                                                          op=ALU.mult), reads=("xtm", "dtm"), writes=("xdt",))
                    for q in range(8):
                        h0 = q * 4
                        g = q // 4
                        r1, r1tok = rh1.next()
                        ab, abtok = abc.next()
                        et, ettok = ets.next()
                        lt, lttok = lts.next()
                        mt, mttok = mts.next()
                        cd, cdtok = cds.next()
                        P.op("dve", lambda e, r1=r1, h0=h0: e.tensor_tensor(
                            out=r1[:], in0=bc_mid(trif[:, :], 4), in1=bc_last(dtm[:, 32 + h0:32 + h0 + 4], 128),
                            op=ALU.mult), reads=("trif", "dtm"), writes=(r1tok,))
                        P.op("pool", lambda e, ab=ab, h0=h0: e.tensor_copy(
                            out=ab[:], in_=bc_last(dtm[:, 32 + h0:32 + h0 + 4], 128)), reads=("dtm",),
                            writes=(abtok,))
                        P.op("pe", lambda e, r1=r1: e.matmul(psC[:, :], lhsT=onesf[:],
                                                             rhs=r1[:].rearrange("p h l -> p (h l)"), start=True,
                                                             stop=True), reads=(r1tok, "onesf"), writes=("psC",))

                        def lmm(e, r1=r1, ab=ab):
                            e.matmul(psL[:, :], lhsT=onesf[:], rhs=r1[:].rearrange("p h l -> p (h l)"), start=True,
                                     stop=False)
                            e.matmul(psL[:, :], lhsT=ntri[:], rhs=ab[:].rearrange("p h l -> p (h l)"), start=False,
                                     stop=False)
                            return e.matmul(psL[:, :], lhsT=idf[:], rhs=nmask[:].rearrange("p h l -> p (h l)"),
                                            start=False, stop=True)
                        P.op("pe", lmm, reads=(r1tok, abtok, "onesf", "ntri", "idf", "nmask"), writes=("psL",))
                        P.op("act", lambda e, et=et: e.activation(out=et[:].rearrange("p h l -> p (h l)"),
                                                                  in_=psC[:, :], func=AF.Exp),
                             reads=("psC",), writes=(ettok,))
                        P.op("act", lambda e, lt=lt: e.activation(out=lt[:].rearrange("p h l -> p (h l)"),
                                                                  in_=psL[:, :], func=AF.Exp),
                             reads=("psL",), writes=(lttok,))
                        P.op("pool", lambda e, et=et, h0=h0: e.tensor_copy(out=elast[:, h0:h0 + 4], in_=et[:, :, 127]),
                             reads=(ettok,), writes=("elast",))
                        P.op("dve", lambda e, mt=mt, lt=lt, g=g: e.tensor_tensor(
                            out=mt[:], in0=lt[:], in1=bc_mid(gt[:, g, :], 4), op=ALU.mult),
                            reads=(lttok, "gt"), writes=(mttok,))
                        P.op("pool", lambda e, cd=cd, et=et, g=g, xcs=xcs, c0=c0: e.tensor_tensor(
                            out=cd[:], in0=et[:], in1=bc_mid(xcs[:, 18 + g, c0:c0 + 128], 4), op=ALU.mult),
                            reads=(ettok, xcstok), writes=(cdtok,))
                        P.op("dve", lambda e, lt=lt, h0=h0: e.tensor_tensor(
                            out=xd2[:, h0:h0 + 4, :], in0=xdt[:, h0:h0 + 4, :], in1=bc_last(lt[:, :, 127], 64),
                            op=ALU.mult), reads=(lttok, "xdt"), writes=("xd2",))
                        if q % 2 == 0:
                            psY, ytok = psYs.next()

                        def ymm(e, mt=mt, cd=cd, h0=h0, psY=psY):
                            inst = None
                            for hh in range(4):
                                h = h0 + hh
                                pr = (h % 2) * 64
                                cl = ((h // 2) % 4) * 128
                                e.matmul(psY[pr:pr + 64, cl:cl + 128], lhsT=xdt[:, h, :], rhs=mt[:, hh, :],
                                         start=True, stop=False)
                                inst = e.matmul(psY[pr:pr + 64, cl:cl + 128], lhsT=Sbf[:, h * 64:(h + 1) * 64],
                                                rhs=cd[:, hh, :], start=False, stop=True)
                            return inst
                        P.op("pe", ymm, reads=(mttok, cdtok, "xdt", "Sbf"), writes=(ytok,))
                        if q % 2 == 1:
                            for jj in range(4):
                                j = (q // 2) * 4 + jj
                                P.op("dve", lambda e, j=j, jj=jj, psY=psY, xcs=xcs, c0=c0: e.scalar_tensor_tensor(
                                    out=ysp[:, j, c0:c0 + 128], in0=xcs[:, j, c0:c0 + 128], scalar=dct[:, j:j + 1],
                                    in1=psY[:, jj * 128:(jj + 1) * 128], op0=ALU.mult, op1=ALU.add),
                                    reads=(ytok, xcstok, "dct"), writes=("ysp",))
                    for g in range(2):
                        def smm(e, g=g):
                            inst = None
                            for i in range(2):
                                inst = e.matmul(psS[i][:, :], lhsT=btm[:, g, :],
                                                rhs=xd2[:, g * 16 + i * 8:g * 16 + i * 8 + 8, :].rearrange(
                                                    "p h q -> p (h q)"), start=True, stop=True)
                            return inst
                        P.op("pe", smm, reads=("btm", "xd2"), writes=("psS",))
                        P.op("dve", lambda e, g=g: e.tensor_tensor(
                            out=Stmp[:].rearrange("p (h q) -> p h q", q=64),
                            in0=S[:, g * 1024:(g + 1) * 1024].rearrange("p (h q) -> p h q", q=64),
                            in1=bc_last(elast[:, g * 16:(g + 1) * 16], 64), op=ALU.mult),
                            reads=("S", "elast"), writes=("Stmp",))
                        for i in range(2):
                            P.op("dve", lambda e, g=g, i=i: e.tensor_tensor(
                                out=S[:, g * 1024 + i * 512:g * 1024 + (i + 1) * 512],
                                in0=Stmp[:, i * 512:(i + 1) * 512], in1=psS[i][:, :], op=ALU.add),
                                reads=("Stmp", "psS"), writes=("S",))
                        P.op("pool", lambda e, g=g: e.tensor_copy(out=Sbf[:, g * 1024:(g + 1) * 1024],
                                                                  in_=S[:, g * 1024:(g + 1) * 1024]),
                             reads=("S",), writes=("Sbf",))
                P.dma("sp", lambda e, s0=s0: e.dma_start(out=szs[:], in_=szv[:, :, s0:s0 + 512]), writes=("szs",))
                for j in range(16):
                    g = j // 8
                    sq, sqtok = sqs.next()
                    P.op("dve", lambda e, j=j: e.tensor_tensor(out=ysp[:, j, :], in0=ysp[:, j, :], in1=szs[:, j, :],
                                                               op=ALU.mult), reads=("ysp", "szs"), writes=("ysp",))
                    P.op("act", lambda e, j=j, sq=sq: e.activation(out=sq[:], in_=ysp[:, j, :], func=AF.Square),
                         reads=("ysp",), writes=(sqtok,))
                    P.op("pe", lambda e, j=j, sq=sq, g=g: e.matmul((psC if g == 0 else psL)[:, :], lhsT=ones[:],
                                                                   rhs=sq[:], start=(j % 8 == 0), stop=(j % 8 == 7)),
                         reads=(sqtok, "ones"), writes=("psC" if g == 0 else "psL",))
                for g in range(2):
                    P.op("act", lambda e, g=g: e.activation(out=rsg[:, g, :], in_=(psC if g == 0 else psL)[:, :],
                                                            func=AF.Sqrt, bias=EPS, scale=1.0 / 1024.0),
                         reads=("psC" if g == 0 else "psL",), writes=("rsg",))
                P.op("dve", lambda e: e.reciprocal(out=rsg[:], in_=rsg[:]), reads=("rsg",), writes=("rsg",))
                for j in range(16):
                    P.op("dve", lambda e, j=j: e.scalar_tensor_tensor(
                        out=yno[:, j, :], in0=ysp[:, j, :], scalar=nwt[:, j:j + 1], in1=rsg[:, j // 8, :],
                        op0=ALU.mult, op1=ALU.mult), reads=("ysp", "rsg", "nwt"), writes=("yno",))
                P.dma("sp", lambda e, s0=s0: e.dma_start(out=ynv[:, :, s0:s0 + 512], in_=yno[:]), reads=("yno",),
                      writes=(("ynT", s0),))
            P.flush()


def mamba_inputs(xT_b, r, nmw, w_in, conv_w, conv_b, dt_bias, a_log, d_skip, norm_w):
    GN = 8 * 128
    zc = np.arange(r * 2048, (r + 1) * 2048)
    xc = DIN + zc
    bcn = 2 * DIN + np.arange(r * 256, (r + 1) * 256)
    ccn = 2 * DIN + GN + np.arange(r * 256, (r + 1) * 256)
    dc = 2 * DIN + 2 * GN + np.arange(r * 32, (r + 1) * 32)
    wmain = w_in[:, np.concatenate([zc, xc, bcn, ccn])]
    wdt = np.zeros((D, 128), np.float32)
    wdt[:, 0:32] = w_in[:, dc]
    wdt[:, 32:64] = w_in[:, dc]
    wt = tile_w(np.concatenate([wmain, wdt], axis=1))
    cch = np.concatenate([zc, DIN + np.arange(r * 256, (r + 1) * 256), DIN + GN + np.arange(r * 256, (r + 1) * 256)])
    cwm = np.ascontiguousarray(conv_w[:, cch].T.reshape(20, 128, 4).transpose(1, 0, 2))
    cbm = col_pb(conv_b[cch])
    hs = np.arange(r * 32, (r + 1) * 32)
    dtb = np.concatenate([dt_bias[hs], dt_bias[hs]]).reshape(64, 1).astype(np.float32)
    alc = np.concatenate([np.zeros(32, np.float32), a_log[hs]]).reshape(64, 1).astype(np.float32)
    sgn = np.concatenate([np.ones(32, np.float32), -np.ones(32, np.float32)]).reshape(64, 1)
    dcl = col_pb(np.repeat(d_skip[hs], 64))
    nwm = col_pb(norm_w[zc])
    return {"xT": xT_b, "nm": col_pb(nmw), "w_in": wt, "cwm": cwm, "cbm": cbm, "dtb": dtb, "alc": alc, "sgn": sgn,
            "dcl": dcl, "nwm": nwm}


TH = SEQ + HALO


def build_full(layers=(0, 1, 2, 3), final=True):
    T = SEQ
    nc = bass.Bass("TRN2", target_bir_lowering=False)
    ext = lambda name, shape, dt=F32: nc.dram_tensor(name, shape, dt, kind="ExternalInput").ap()
    itn = lambda name, shape, dt=F32: nc.dram_tensor(name, shape, dt, kind="Internal").ap()
    x0 = ext("x0", [D, TH])
    pos = ext("pos", [1, T], mybir.dt.int32)
    invf = ext("invf", [128, 1])
    rmat = ext("rmat", [128, 128])
    masks = ext("masks", [128, 4, 512], BF16)
    sgn = ext("sgn", [64, 1])
    fin = ext("fin", [128, 32])
    W = {}
    for i in layers:
        p = "l%d_" % i
        W[p + "nm"] = ext(p + "nm", [128, 32])
        if i % 2 == 0:
            W[p + "w_in"] = ext(p + "w_in", [4, NBLK_IN, 128, 32, 128])
            W[p + "cwm"] = ext(p + "cwm", [4, 128, 20, 4])
            W[p + "cbm"] = ext(p + "cbm", [4, 128, 20])
            W[p + "dtb"] = ext(p + "dtb", [4, 64, 1])
            W[p + "alc"] = ext(p + "alc", [4, 64, 1])
            W[p + "dcl"] = ext(p + "dcl", [4, 128, 16])
            W[p + "nwm"] = ext(p + "nwm", [4, 128, 16])
            KCo = 64
        else:
            W[p + "w_qk"] = ext(p + "w_qk", [4, 16, 128, 32, 128])
            W[p + "w_v"] = ext(p + "w_v", [4, 2, 128, 32, 512])
            W[p + "lqk"] = ext(p + "lqk", [128, 4])
            W[p + "sub"] = ext(p + "sub", [128, 2])
            KCo = 32
        W[p + "w_o"] = ext(p + "w_o", [32, 128, KCo, 128])
        W[p + "nf"] = ext(p + "nf", [128, 32])
        W[p + "w_up"] = ext(p + "w_up", [2 * NFB, 128, 32, 128])
        W[p + "cw"] = ext(p + "cw", [128, 2 * NFB, 3])
        W[p + "cb"] = ext(p + "cb", [128, 2 * NFB])
        for g in range(4):
            W[p + "w_dn%d" % g] = ext(p + "w_dn%d" % g, [32, 128, GROUPS[g], 128])
    out = nc.dram_tensor("out", [D, T], F32, kind="ExternalOutput").ap()
    xa = itn("xa", [D, TH])
    xb = itn("xb", [D, TH])
    ynT = itn("ynT", [DIN, TH], BF16)
    uT = itn("uT", [D, T], BF16)
    szT = itn("szT", [2048, T])
    xbcT = itn("xbcT", [2560, T])
    xcT = itn("xcT", [2560, T], BF16)
    dtaT = itn("dtaT", [64, T])
    qk = itn("qk", [16, 128, T], BF16)
    vtm = itn("vtm", [T, HPC * 256], BF16)
    xm = itn("xm", [D, TLH])

    with ExitStack() as es0, nc.allow_low_precision("bf16 matmul operands, fp32 accumulation"):
        P = Prog(nc, es0)
        with ExitStack() as es:
            zf = es.enter_context(nc.sbuf_tensor("zf", [128, 64, HALO], F32))
            zb = es.enter_context(nc.sbuf_tensor("zb", [128, 64, HALO], BF16))
            P.op("dve", lambda e: e.memset(zf[:], 0.0), writes=("zf",))
            P.op("dve", lambda e: e.memset(zb[:], 0.0), writes=("zb",))
            for nm_, buf in (("xa", xa), ("xb", xb)):
                P.dma("sp", lambda e, buf=buf: e.dma_start(
                    out=buf[:, 0:HALO].rearrange("(k p) t -> p k t", p=128), in_=zf[:, 0:32, :]),
                    reads=("zf",), writes=(nm_,))
            P.dma("sp", lambda e: e.dma_start(out=ynT[:, 0:HALO].rearrange("(k p) t -> p k t", p=128), in_=zb[:]),
                  reads=("zb",), writes=("ynT",))
            P.flush()
        cur = x0
        ping = [xb, xa]
        for li, i in enumerate(layers):
            p = "l%d_" % i
            last = (li == len(layers) - 1)
            nxt = ping[li % 2]
            xfull = cur[:, HALO:TH]
            if i % 2 == 0:
                KCo = 64
                for r in range(4):
                    io = dict(xT=xfull, nm=W[p + "nm"], w_in=W[p + "w_in"][r], cwm=W[p + "cwm"][r],
                              cbm=W[p + "cbm"][r], dtb=W[p + "dtb"][r], alc=W[p + "alc"][r], sgn=sgn,
                              dcl=W[p + "dcl"][r], nwm=W[p + "nwm"][r],
                              ynT=ynT[r * 2048:(r + 1) * 2048, HALO:TH], uT=uT, szT=szT, xbcT=xbcT, xcT=xcT,
                              dtaT=dtaT)
                    emit_mamba(nc, P, io, do_norm=(r == 0))
            else:
                KCo = 32
                lam_init = 0.8 - 0.6 * math.exp(-0.3 * i)
                for r in range(4):
                    io = dict(xT=xfull, nm=W[p + "nm"], w_qk=W[p + "w_qk"][r], w_v=W[p + "w_v"][r], pos=pos,
                              invf=invf, rmat=rmat, masks=masks, lqk=W[p + "lqk"], sub=W[p + "sub"],
                              onT=ynT[r * 1024:(r + 1) * 1024, HALO:TH], uT=uT, qk=qk, vtm=vtm)
                    emit_attn(nc, P, io, lam_init, do_norm=(r == 0))
            for r in range(4):
                c0 = r * TL
                dst = out[:, c0:c0 + TL] if last else nxt[:, HALO + c0:HALO + c0 + TL]
                io = dict(xT=cur[:, c0:c0 + TLH], ynT=ynT[0:KCo * 128, c0:c0 + TLH], w_o=W[p + "w_o"],
                          nf=W[p + "nf"], w_up=W[p + "w_up"], cw=W[p + "cw"], cb=W[p + "cb"],
                          w_dn=[W[p + "w_dn%d" % g] for g in range(4)], fin=fin, out=dst, xm=xm)
                emit_row(nc, P, io, KCo, final and last)
            cur = nxt
    return nc


def full_inputs(inp, layers=(0, 1, 2, 3)):
    f32 = lambda a: np.ascontiguousarray(np.asarray(a), dtype=np.float32)
    invf, rm, masks = rope_consts()
    base = {"invf": invf, "rmat": rm, "masks": masks,
            "sgn": np.concatenate([np.ones(32, np.float32), -np.ones(32, np.float32)]).reshape(64, 1),
            "fin": col_pb(f32(inp["final_norm"]))}
    for i in layers:
        p = "l%d_" % i
        nmw = f32(inp[p + "norm_mix"])
        base[p + "nm"] = col_pb(nmw)
        if i % 2 == 0:
            w_in = f32(inp[p + "m_w_in"])
            args = [f32(inp[p + k]) for k in ("m_conv_w", "m_conv_b", "m_dt_bias", "m_a_log", "m_d", "m_norm")]
            per_r = [mamba_inputs(None, r, nmw, w_in, *args) for r in range(4)]
            for src, dst in (("w_in", "w_in"), ("cwm", "cwm"), ("cbm", "cbm"), ("dtb", "dtb"), ("alc", "alc"),
                             ("dcl", "dcl"), ("nwm", "nwm")):
                base[p + dst] = np.ascontiguousarray(np.stack([per_r[r][src] for r in range(4)], axis=0))
            w_out = f32(inp[p + "m_w_out"])
            del w_in, per_r
        else:
            w_qkv = f32(inp[p + "a_w_qkv"])
            args = [f32(inp[p + k]) for k in ("a_lq1", "a_lk1", "a_lq2", "a_lk2", "a_subln")]
            per_r = [attn_inputs(None, np.zeros(SEQ, np.int32), r, nmw, w_qkv, *args) for r in range(4)]
            base[p + "w_qk"] = np.ascontiguousarray(np.stack([per_r[r]["w_qk"] for r in range(4)], axis=0))
            base[p + "w_v"] = np.ascontiguousarray(np.stack([per_r[r]["w_v"] for r in range(4)], axis=0))
            base[p + "lqk"] = per_r[0]["lqk"]
            base[p + "sub"] = per_r[0]["sub"]
            w_out = f32(inp[p + "a_w_o"])
            del w_qkv, per_r
        ri = row_inputs(np.zeros((1, 1), np.float32), np.zeros((1, 1), NPBF), w_out, f32(inp[p + "norm_ffn"]),
                        f32(inp[p + "f_w_up"]), f32(inp[p + "f_conv_w"]), f32(inp[p + "f_conv_b"]),
                        f32(inp[p + "f_w_down"]), f32(inp["final_norm"]))
        for k in ("w_o", "nf", "w_up", "cw", "cb", "w_dn0", "w_dn1", "w_dn2", "w_dn3"):
            base[p + k] = ri[k]
        del ri, w_out
    x = f32(inp["x"])
    positions = np.asarray(inp["positions"]).astype(np.int32)
    maps = []
    for b in range(x.shape[0]):
        m = dict(base)
        x0 = np.zeros((D, TH), np.float32)
        x0[:, HALO:] = x[b].T
        m["x0"] = x0
        m["pos"] = np.ascontiguousarray(positions[b].reshape(1, -1))
        maps.append(m)
    return maps


INPUT_NAMES = (
    "x",
    "positions",
    "l0_norm_mix",
    "l0_m_w_in",
    "l0_m_conv_w",
    "l0_m_conv_b",
    "l0_m_dt_bias",
    "l0_m_a_log",
    "l0_m_d",
    "l0_m_norm",
    "l0_m_w_out",
    "l0_norm_ffn",
    "l0_f_w_up",
    "l0_f_conv_w",
    "l0_f_conv_b",
    "l0_f_w_down",
    "l1_norm_mix",
    "l1_a_w_qkv",
    "l1_a_lq1",
    "l1_a_lk1",
    "l1_a_lq2",
    "l1_a_lk2",
    "l1_a_subln",
    "l1_a_w_o",
    "l1_norm_ffn",
    "l1_f_w_up",
    "l1_f_conv_w",
    "l1_f_conv_b",
    "l1_f_w_down",
    "l2_norm_mix",
    "l2_m_w_in",
    "l2_m_conv_w",
    "l2_m_conv_b",
    "l2_m_dt_bias",
    "l2_m_a_log",
    "l2_m_d",
    "l2_m_norm",
    "l2_m_w_out",
    "l2_norm_ffn",
    "l2_f_w_up",
    "l2_f_conv_w",
    "l2_f_conv_b",
    "l2_f_w_down",
    "l3_norm_mix",
    "l3_a_w_qkv",
    "l3_a_lq1",
    "l3_a_lk1",
    "l3_a_lq2",
    "l3_a_lk2",
    "l3_a_subln",
    "l3_a_w_o",
    "l3_norm_ffn",
    "l3_f_w_up",
    "l3_f_conv_w",
    "l3_f_conv_b",
    "l3_f_w_down",
    "final_norm",
)

_NC_FULL = {}


def kernel(**inp):
    missing = [n for n in INPUT_NAMES if n not in inp]
    assert not missing, "missing inputs: %s" % missing
    if "full" not in _NC_FULL:
        _NC_FULL["full"] = build_full()
    nc = _NC_FULL["full"]
    maps = full_inputs(inp)
    res = run_bass_kernel_spmd(nc, maps, core_ids=list(range(len(maps))))
    out = np.stack([res.results[b]["out"].T for b in range(len(maps))], axis=0)
    return np.ascontiguousarray(out, dtype=np.float32)
```

```python
import math
from contextlib import ExitStack

import numpy as np
import ml_dtypes

import concourse.bass as bass
import concourse.mybir as mybir
from concourse.bass_utils import run_bass_kernel_spmd

F32 = mybir.dt.float32
BF16 = mybir.dt.bfloat16
AF = mybir.ActivationFunctionType
ALU = mybir.AluOpType
NPBF = ml_dtypes.bfloat16

D = 4096
SEQ = 4096
NB = 2
DEPTH = 4
EPS = 1e-5
DFF = 11008
NFB = DFF // 128
DIN = 8192
TL = 1024
HALO = 2
TLH = TL + HALO
TT3 = [(0, 342), (342, 342), (684, 342)]
GROUPS = [22, 22, 21, 21]

ENGS = ["pe", "act", "dve", "pool", "sp"]
NDMA = {"sp": 12, "pool": 6, "act": 4}


class Prog:
    def __init__(self, nc, es):
        self.nc = nc
        self.sem = {}
        for e in ENGS:
            self.sem[("e", e)] = es.enter_context(nc.semaphore("s_" + e))
        for q, n in NDMA.items():
            for i in range(n):
                self.sem[("d", q, i)] = es.enter_context(nc.semaphore("d_%s%d" % (q, i)))
        self.sem[("ph",)] = es.enter_context(nc.semaphore("s_phase"))
        self.sem[("cc",)] = es.enter_context(nc.semaphore("s_cc"))
        self.ncc = 0
        self.cnt = {e: 0 for e in ENGS}
        self.dval = {k: 0 for k in self.sem if k[0] == "d"}
        self.ndma = {q: 0 for q in NDMA}
        self.nphase = 0
        self._reset()

    def _reset(self):
        self.ops = {e: [] for e in ENGS}
        self.state = {}
        self.waited = {e: {} for e in ENGS}

    def _deps(self, eng, reads, writes):
        need = {}

        def add(ev):
            if ev is None:
                return
            k, v = ev
            if need.get(k, 0) < v:
                need[k] = v

        for key in reads:
            st = self.state.get(key)
            if st:
                add(st[0])
        for key in writes:
            st = self.state.get(key)
            if st:
                add(st[0])
                for k, v in st[1].items():
                    add((k, v))
        out = []
        w = self.waited[eng]
        if eng == "pe":
            need.pop(("e", "pe"), None)
        for k, v in need.items():
            if w.get(k, 0) < v:
                w[k] = v
                out.append((k, v))
        return out

    def _mark(self, ev, reads, writes):
        for key in reads:
            st = self.state.setdefault(key, [None, {}])
            if st[1].get(ev[0], 0) < ev[1]:
                st[1][ev[0]] = ev[1]
        for key in writes:
            self.state[key] = [ev, {}]

    def op(self, eng, fn, reads=(), writes=()):
        waits = self._deps(eng, reads, writes)
        self.cnt[eng] += 1
        ev = (("e", eng), self.cnt[eng])
        self.ops[eng].append((waits, fn, ev[0], 1))
        self._mark(ev, reads, writes)

    def dma(self, q, fn, reads=(), writes=()):
        waits = self._deps(q, reads, writes)
        i = self.ndma[q] % NDMA[q]
        self.ndma[q] += 1
        sk = ("d", q, i)
        prev = self.dval[sk]
        if prev > 0 and self.waited[q].get(sk, 0) < prev:
            self.waited[q][sk] = prev
            waits.append((sk, prev))
        self.dval[sk] = prev + 16
        ev = (sk, prev + 16)
        self.ops[q].append((waits, fn, sk, 16))
        self._mark(ev, reads, writes)

    def cc(self, fn, reads=(), writes=()):
        waits = self._deps("pool", reads, writes)
        self.ncc += 1
        ev = (("cc",), self.ncc)
        self.ops["pool"].append((waits, fn, ("cc",), 1))
        self._mark(ev, reads, writes)

    def flush(self):
        nc = self.nc
        self.nphase += 1
        ph = self.nphase
        final_cnt = dict(self.cnt)
        final_d = dict(self.dval)
        final_cc = self.ncc
        sem = self.sem
        ops = self.ops

        def run(engname, eng):
            for waits, fn, sk, inc in ops[engname]:
                for k, v in waits:
                    eng.wait_ge(sem[k], v)
                inst = fn(eng)
                inst.then_inc(sem[sk], inc)
            if engname == "sp":
                for e in ENGS:
                    if final_cnt[e] > 0:
                        eng.wait_ge(sem[("e", e)], final_cnt[e])
                for k, v in final_d.items():
                    if v > 0:
                        eng.wait_ge(sem[k], v)
                if final_cc > 0:
                    eng.wait_ge(sem[("cc",)], final_cc)
                eng.sem_inc(sem[("ph",)], 1)
            eng.wait_ge(sem[("ph",)], ph)

        with nc.Block() as block:
            @block.sync
            def _(e):
                run("sp", e)

            @block.tensor
            def _(e):
                run("pe", e)

            @block.scalar
            def _(e):
                run("act", e)

            @block.vector
            def _(e):
                run("dve", e)

            @block.gpsimd
            def _(e):
                run("pool", e)
        self._reset()


class NS:
    _uid = [0]

    def __init__(self, nc, tag=""):
        self.nc = nc
        NS._uid[0] += 1
        self.sfx = "_%s%d" % (tag, NS._uid[0])

    def sbuf_tensor(self, name, shape, dt):
        return self.nc.sbuf_tensor(name + self.sfx, shape, dt)

    def psum_tensor(self, name, shape, dt):
        return self.nc.psum_tensor(name + self.sfx, shape, dt)


def _eng(nc, name):
    return {"pe": nc.tensor, "act": nc.scalar, "dve": nc.vector, "pool": nc.gpsimd, "sp": nc.sync}[name]


class Ring:
    def __init__(self, items):
        self.items = items
        self.i = 0

    def next(self):
        it = self.items[self.i % len(self.items)]
        self.i += 1
        return it


def gemm_T(P, wdram, nblocks, KC, xin, xin_tok, ttiles, wslots, psring, epilogue, mrows=None, blk_of=None):
    for bi in range(nblocks):
        wb, wtok = wslots.next()
        src = wdram[blk_of(bi) if blk_of else bi]
        P.dma("pool", lambda e, wb=wb, src=src: e.dma_start(out=wb[:], in_=src), reads=(), writes=(wtok,))
        m = 128 if mrows is None else mrows(bi)
        for ti, (t0, tw) in enumerate(ttiles):
            ps, ptok = psring.next()

            def mm(e, wb=wb, ps=ps, t0=t0, tw=tw, m=m):
                inst = None
                for kc in range(KC):
                    inst = e.matmul(ps[:m, :tw], lhsT=wb[:, kc, :m], rhs=xin[:, kc, t0:t0 + tw],
                                    start=(kc == 0), stop=(kc == KC - 1))
                return inst

            P.op("pe", mm, reads=(wtok, xin_tok), writes=(ptok,))
            epilogue(bi, ti, (t0, tw), ps, ptok)


def build_row(KCo, final):
    nc = bass.Bass("TRN2", target_bir_lowering=False)
    io = dict(
        xT=nc.dram_tensor("xT", [D, TLH], F32, kind="ExternalInput").ap(),
        ynT=nc.dram_tensor("ynT", [KCo * 128, TLH], BF16, kind="ExternalInput").ap(),
        w_o=nc.dram_tensor("w_o", [32, 128, KCo, 128], F32, kind="ExternalInput").ap(),
        nf=nc.dram_tensor("nf", [128, 32], F32, kind="ExternalInput").ap(),
        w_up=nc.dram_tensor("w_up", [2 * NFB, 128, 32, 128], F32, kind="ExternalInput").ap(),
        cw=nc.dram_tensor("cw", [128, 2 * NFB, 3], F32, kind="ExternalInput").ap(),
        cb=nc.dram_tensor("cb", [128, 2 * NFB], F32, kind="ExternalInput").ap(),
        w_dn=[nc.dram_tensor("w_dn%d" % g, [32, 128, GROUPS[g], 128], F32, kind="ExternalInput").ap()
              for g in range(4)],
        fin=nc.dram_tensor("fin", [128, 32], F32, kind="ExternalInput").ap(),
        out=nc.dram_tensor("out", [D, TL], F32, kind="ExternalOutput").ap(),
        xm=nc.dram_tensor("xm", [D, TLH], F32, kind="Internal").ap(),
    )
    with ExitStack() as es0, nc.allow_low_precision("bf16 matmul operands, fp32 accumulation"):
        P = Prog(nc, es0)
        emit_row(nc, P, io, KCo, final)
    return nc


def emit_row(nc_real, P, io, KCo, final, tag="r"):
    nc = NS(nc_real, tag)
    xT, ynT, w_o, nf, w_up, cw, cb, w_dn, fin, out, xm = (io[k] for k in (
        "xT", "ynT", "w_o", "nf", "w_up", "cw", "cb", "w_dn", "fin", "out", "xm"))
    with ExitStack() as es:
        pst = [es.enter_context(nc.psum_tensor("ps%d" % i, [128, 512], F32)) for i in range(8)]
        ones = es.enter_context(nc.sbuf_tensor("ones", [128, 128], BF16))
        nft = es.enter_context(nc.sbuf_tensor("nft", [128, 32], F32))
        fint = es.enter_context(nc.sbuf_tensor("fint", [128, 32], F32))
        cwt = es.enter_context(nc.sbuf_tensor("cwt", [128, 2 * NFB, 3], F32))
        cbt = es.enter_context(nc.sbuf_tensor("cbt", [128, 2 * NFB], F32))
        rstd = es.enter_context(nc.sbuf_tensor("rstd", [128, TLH], F32))

        P.op("dve", lambda e: e.memset(ones[:], 1.0), writes=("ones",))
        P.dma("sp", lambda e: e.dma_start(out=nft[:], in_=nf), writes=("nft",))
        P.dma("sp", lambda e: e.dma_start(out=fint[:], in_=fin), writes=("fint",))
        P.dma("sp", lambda e: e.dma_start(out=cwt[:], in_=cw), writes=("cwt",))
        P.dma("sp", lambda e: e.dma_start(out=cbt[:], in_=cb), writes=("cbt",))

        with ExitStack() as ph:
            yn = ph.enter_context(nc.sbuf_tensor("yn", [128, KCo, TLH], BF16))
            wsl = Ring([(ph.enter_context(nc.sbuf_tensor("wo%d" % i, [128, KCo, 128], BF16)), "wo%d" % i)
                        for i in range(2)])
            xts = Ring([(ph.enter_context(nc.sbuf_tensor("xt%d" % i, [128, TLH], F32)), "xt%d" % i)
                        for i in range(2)])
            sqs = Ring([(ph.enter_context(nc.sbuf_tensor("sq%d" % i, [128, TLH], BF16)), "sq%d" % i)
                        for i in range(2)])
            psr = Ring([(pst[i], "ps%d" % i) for i in range(4)])
            ynv = ynT.rearrange("(kc p) t -> p kc t", p=128)
            half = KCo // 2
            P.dma("sp", lambda e: e.dma_start(out=yn[:, :half, :], in_=ynv[:, :half, :]), writes=("yn",))
            P.dma("sp", lambda e: e.dma_start(out=yn[:, half:, :], in_=ynv[:, half:, :]), writes=("yn",))
            cur = {}

            def epi1(bi, ti, tt, ps, ptok):
                t0, tw = tt
                if ti == 0:
                    xt, xtok = xts.next()
                    sq, sqtok = sqs.next()
                    cur["x"] = (xt, xtok, sq, sqtok)
                    P.dma("sp", lambda e, xt=xt, bi=bi: e.dma_start(out=xt[:], in_=xT[bi * 128:(bi + 1) * 128, :]),
                          writes=(xtok,))
                xt, xtok, sq, sqtok = cur["x"]
                P.op("dve", lambda e: e.tensor_tensor(out=xt[:, t0:t0 + tw], in0=xt[:, t0:t0 + tw],
                                                      in1=ps[:, :tw], op=ALU.add),
                     reads=(ptok, xtok), writes=(xtok,))
                if ti == 2:
                    P.dma("sp", lambda e: e.dma_start(out=xm[bi * 128:(bi + 1) * 128, :], in_=xt[:]),
                          reads=(xtok,), writes=(("xm", bi),))
                    P.op("act", lambda e: e.activation(out=sq[:], in_=xt[:], func=AF.Square),
                         reads=(xtok,), writes=(sqtok,))

                    def ssmm(e):
                        inst = None
                        for j, (a, w) in enumerate(TT3):
                            inst = e.matmul(pst[4 + j][:, :w], lhsT=ones[:], rhs=sq[:, a:a + w],
                                            start=(bi == 0), stop=(bi == 31))
                        return inst

                    P.op("pe", ssmm, reads=(sqtok, "ones"), writes=("ss",))

            gemm_T(P, w_o, 32, KCo, yn, "yn", TT3, wsl, psr, epi1)
            for j, (a, w) in enumerate(TT3):
                P.op("act", lambda e, j=j, a=a, w=w: e.activation(out=rstd[:, a:a + w], in_=pst[4 + j][:, :w],
                                                                  func=AF.Sqrt, bias=EPS, scale=1.0 / D),
                     reads=("ss",), writes=("rstd",))
            P.op("dve", lambda e: e.reciprocal(out=rstd[:], in_=rstd[:]), reads=("rstd",), writes=("rstd",))
            P.flush()

        u2 = es.enter_context(nc.sbuf_tensor("u2", [128, 32, TLH], BF16))
        with ExitStack() as ph:
            xts = Ring([(ph.enter_context(nc.sbuf_tensor("xu%d" % i, [128, TLH], F32)), "xu%d" % i)
                        for i in range(3)])
            for bi in range(32):
                xt, xtok = xts.next()
                P.dma("sp", lambda e, xt=xt, bi=bi: e.dma_start(out=xt[:], in_=xm[bi * 128:(bi + 1) * 128, :]),
                      writes=(xtok,))
                P.op("dve", lambda e, xt=xt, bi=bi: e.scalar_tensor_tensor(
                    out=u2[:, bi, :], in0=xt[:], scalar=nft[:, bi:bi + 1], in1=rstd[:], op0=ALU.mult, op1=ALU.mult),
                    reads=(xtok, "nft", "rstd"), writes=("u2",))
            P.flush()

        with ExitStack() as ph:
            act = ph.enter_context(nc.sbuf_tensor("actb", [128, 22, TL], BF16))
            wsl = Ring([(ph.enter_context(nc.sbuf_tensor("wu%d" % i, [128, 32, 128], BF16)), "wu%d" % i)
                        for i in range(4)])
            wds = Ring([(ph.enter_context(nc.sbuf_tensor("wd%d" % i, [128, 22, 128], BF16)), "wd%d" % i)
                        for i in range(2)])
            hts = Ring([(ph.enter_context(nc.sbuf_tensor("ht%d" % i, [128, TLH], F32)), "ht%d" % i)
                        for i in range(3)])
            cgs = Ring([(ph.enter_context(nc.sbuf_tensor("cg%d" % i, [128, TL], F32)), "cg%d" % i)
                        for i in range(2)])
            cvs = Ring([(ph.enter_context(nc.sbuf_tensor("cv%d" % i, [128, TL], F32)), "cv%d" % i)
                        for i in range(2)])
            xas = Ring([(ph.enter_context(nc.sbuf_tensor("xa%d" % i, [128, TL], F32)), "xa%d" % i)
                        for i in range(2)])
            psr = Ring([(pst[i], "ps%d" % i) for i in range(6)])
            psd = Ring([(pst[6 + i], "ps%d" % (6 + i)) for i in range(2)])
            goff = 0
            for g in range(4):
                CG = GROUPS[g]
                st = {}

                def blk_of(bi, goff=goff):
                    j, isv = bi // 2, bi % 2
                    return (NFB if isv else 0) + goff + j

                def epi3(bi, ti, tt, ps, ptok, goff=goff):
                    t0, tw = tt
                    j, isv = bi // 2, bi % 2
                    fb = blk_of(bi)
                    if ti == 0:
                        st["h"] = hts.next()
                    ht, htok = st["h"]
                    P.op("act", lambda e: e.activation(out=ht[:, t0:t0 + tw], in_=ps[:, :tw], func=AF.Copy),
                         reads=(ptok,), writes=(htok,))
                    if ti != 2:
                        return
                    c, ctok = (cvs if isv else cgs).next()
                    P.op("dve", lambda e: e.tensor_scalar(out=c[:], in0=ht[:, 0:TL], scalar1=cwt[:, fb, 0:1],
                                                          scalar2=cbt[:, fb:fb + 1], op0=ALU.mult, op1=ALU.add),
                         reads=(htok, "cwt", "cbt"), writes=(ctok,))
                    for k in (1, 2):
                        P.op("dve", lambda e, k=k: e.scalar_tensor_tensor(
                            out=c[:], in0=ht[:, k:k + TL], scalar=cwt[:, fb, k:k + 1], in1=c[:],
                            op0=ALU.mult, op1=ALU.add), reads=(htok, ctok, "cwt"), writes=(ctok,))
                    if not isv:
                        st["g"] = (c, ctok)
                        return
                    cg, cgtok = st["g"]
                    P.op("act", lambda e: e.activation(out=ht[:, 0:TL], in_=cg[:], func=AF.Silu),
                         reads=(cgtok,), writes=(htok,))
                    P.op("dve", lambda e: e.tensor_tensor(out=act[:, j, :], in0=ht[:, 0:TL], in1=c[:], op=ALU.mult),
                         reads=(htok, ctok), writes=(("act", j),))

                gemm_T(P, w_up, 2 * CG, 32, u2, "u2", TT3, wsl, psr, epi3, blk_of=blk_of)

                for nb in range(32):
                    wd, wdtok = wds.next()
                    P.dma("pool", lambda e, wd=wd, nb=nb, g=g, CG=CG: e.dma_start(out=wd[:, :CG, :], in_=w_dn[g][nb]),
                          writes=(wdtok,))
                    xa, xatok = xas.next()
                    if g == 0:
                        P.dma("sp", lambda e, xa=xa, nb=nb: e.dma_start(
                            out=xa[:], in_=xm[nb * 128:(nb + 1) * 128, HALO:TLH]),
                            reads=(("xm", nb),), writes=(xatok,))
                    else:
                        P.dma("sp", lambda e, xa=xa, nb=nb: e.dma_start(
                            out=xa[:], in_=out[nb * 128:(nb + 1) * 128, :]),
                            reads=(("out", nb),), writes=(xatok,))
                    for hh in range(2):
                        ps, ptok = psd.next()

                        def mmd(e, wd=wd, ps=ps, hh=hh, CG=CG):
                            inst = None
                            for kc in range(CG):
                                inst = e.matmul(ps[:, :], lhsT=wd[:, kc, :], rhs=act[:, kc, hh * 512:(hh + 1) * 512],
                                                start=(kc == 0), stop=(kc == CG - 1))
                            return inst

                        P.op("pe", mmd, reads=(wdtok,) + tuple(("act", j) for j in range(CG)), writes=(ptok,))
                        P.op("dve", lambda e, xa=xa, ps=ps, hh=hh: e.tensor_tensor(
                            out=xa[:, hh * 512:(hh + 1) * 512], in0=xa[:, hh * 512:(hh + 1) * 512], in1=ps[:, :],
                            op=ALU.add), reads=(ptok, xatok), writes=(xatok,))
                    P.dma("sp", lambda e, xa=xa, nb=nb: e.dma_start(out=out[nb * 128:(nb + 1) * 128, :], in_=xa[:]),
                          reads=(xatok,), writes=(("out", nb),))
                goff += CG
            P.flush()

        if final:
            with ExitStack() as ph:
                xts = Ring([(ph.enter_context(nc.sbuf_tensor("xf%d" % i, [128, TL], F32)), "xf%d" % i)
                            for i in range(3)])
                sqs = Ring([(ph.enter_context(nc.sbuf_tensor("sf%d" % i, [128, TL], BF16)), "sf%d" % i)
                            for i in range(2)])
                for bi in range(32):
                    xt, xtok = xts.next()
                    sq, sqtok = sqs.next()
                    P.dma("sp", lambda e, xt=xt, bi=bi: e.dma_start(out=xt[:], in_=out[bi * 128:(bi + 1) * 128, :]),
                          writes=(xtok,))
                    P.op("act", lambda e, xt=xt, sq=sq: e.activation(out=sq[:], in_=xt[:], func=AF.Square),
                         reads=(xtok,), writes=(sqtok,))

                    def ssmm(e, sq=sq, bi=bi):
                        inst = None
                        for j in range(2):
                            inst = e.matmul(pst[j][:, :], lhsT=ones[:], rhs=sq[:, j * 512:(j + 1) * 512],
                                            start=(bi == 0), stop=(bi == 31))
                        return inst

                    P.op("pe", ssmm, reads=(sqtok, "ones"), writes=("ssf",))
                for j in range(2):
                    P.op("act", lambda e, j=j: e.activation(out=rstd[:, j * 512:(j + 1) * 512], in_=pst[j][:, :],
                                                            func=AF.Sqrt, bias=EPS, scale=1.0 / D),
                         reads=("ssf",), writes=("rstd",))
                P.op("dve", lambda e: e.reciprocal(out=rstd[:, :TL], in_=rstd[:, :TL]), reads=("rstd",),
                     writes=("rstd",))
                for bi in range(32):
                    xt, xtok = xts.next()
                    P.dma("sp", lambda e, xt=xt, bi=bi: e.dma_start(out=xt[:], in_=out[bi * 128:(bi + 1) * 128, :]),
                          writes=(xtok,))
                    P.op("dve", lambda e, xt=xt, bi=bi: e.scalar_tensor_tensor(
                        out=xt[:], in0=xt[:], scalar=fint[:, bi:bi + 1], in1=rstd[:, :TL], op0=ALU.mult,
                        op1=ALU.mult), reads=(xtok, "fint", "rstd"), writes=(xtok,))
                    P.dma("sp", lambda e, xt=xt, bi=bi: e.dma_start(out=out[bi * 128:(bi + 1) * 128, :], in_=xt[:]),
                          reads=(xtok,), writes=(("out", bi),))
                P.flush()


def tile_w(W):
    Kd, N = W.shape
    return np.ascontiguousarray(W.reshape(Kd // 128, 128, N // 128, 128).transpose(2, 1, 0, 3))


def col_pb(v):
    return np.ascontiguousarray(v.reshape(-1, 128).T)


def row_inputs(xT, ynT, w_out, nfw, w_up, cw, cb, w_dn, finw):
    ins = {
        "xT": np.ascontiguousarray(xT, dtype=np.float32),
        "ynT": np.ascontiguousarray(ynT),
        "w_o": tile_w(w_out),
        "nf": col_pb(nfw),
        "w_up": tile_w(w_up),
        "cw": np.ascontiguousarray(cw.T.reshape(2 * NFB, 128, 3).transpose(1, 0, 2)),
        "cb": col_pb(cb),
        "fin": col_pb(finw),
    }
    goff = 0
    for g, CG in enumerate(GROUPS):
        ins["w_dn%d" % g] = tile_w(w_dn[goff * 128:(goff + CG) * 128, :])
        goff += CG
    return ins


def norm_phase(P, nc, xT, nmt, nmtok, uT, ones, pst, T):
    xv = xT.rearrange("(kc p) t -> p kc t", p=128)
    uv = uT.rearrange("(kc p) t -> p kc t", p=128)
    with ExitStack() as ph:
        xins = Ring([(ph.enter_context(nc.sbuf_tensor("nx%d" % i, [128, 32, 512], F32)), "nx%d" % i)
                     for i in range(2)])
        uos = Ring([(ph.enter_context(nc.sbuf_tensor("nu%d" % i, [128, 32, 512], BF16)), "nu%d" % i)
                    for i in range(2)])
        sqs = Ring([(ph.enter_context(nc.sbuf_tensor("nq%d" % i, [128, 512], BF16)), "nq%d" % i)
                    for i in range(3)])
        rss = Ring([(ph.enter_context(nc.sbuf_tensor("nr%d" % i, [128, 512], F32)), "nr%d" % i)
                    for i in range(2)])
        pss = Ring([(pst[i], "ps%d" % i) for i in range(2)])
        for tt in range(T // 512):
            t0 = tt * 512
            xin, xtok = xins.next()
            uo, utok = uos.next()
            rs, rtok = rss.next()
            ps, ptok = pss.next()
            P.dma("sp", lambda e, xin=xin, t0=t0: e.dma_start(out=xin[:], in_=xv[:, :, t0:t0 + 512]),
                  writes=(xtok,))
            for bi in range(32):
                sq, sqtok = sqs.next()
                P.op("act", lambda e, sq=sq, xin=xin, bi=bi: e.activation(out=sq[:], in_=xin[:, bi, :],
                                                                           func=AF.Square),
                     reads=(xtok,), writes=(sqtok,))
                P.op("pe", lambda e, sq=sq, ps=ps, bi=bi: e.matmul(ps[:, :], lhsT=ones[:], rhs=sq[:],
                                                                   start=(bi == 0), stop=(bi == 31)),
                     reads=(sqtok, "ones"), writes=(ptok,))
            P.op("act", lambda e, rs=rs, ps=ps: e.activation(out=rs[:], in_=ps[:, :], func=AF.Sqrt, bias=EPS,
                                                            scale=1.0 / D), reads=(ptok,), writes=(rtok,))
            P.op("dve", lambda e, rs=rs: e.reciprocal(out=rs[:], in_=rs[:]), reads=(rtok,), writes=(rtok,))
            for bi in range(32):
                P.op("dve", lambda e, uo=uo, xin=xin, rs=rs, bi=bi: e.scalar_tensor_tensor(
                    out=uo[:, bi, :], in0=xin[:, bi, :], scalar=nmt[:, bi:bi + 1], in1=rs[:], op0=ALU.mult,
                    op1=ALU.mult), reads=(xtok, rtok, nmtok), writes=(utok,))
            P.dma("sp", lambda e, uo=uo, t0=t0: e.dma_start(out=uv[:, :, t0:t0 + 512], in_=uo[:]),
                  reads=(utok,), writes=(("uT", tt),))
        P.flush()


HPC = 4
ATT_SCALE = 128 ** -0.5


def build_attn(lambda_init):
    T = SEQ
    nc = bass.Bass("TRN2", target_bir_lowering=False)
    io = dict(
        xT=nc.dram_tensor("xT", [D, T], F32, kind="ExternalInput").ap(),
        nm=nc.dram_tensor("nm", [128, 32], F32, kind="ExternalInput").ap(),
        w_qk=nc.dram_tensor("w_qk", [16, 128, 32, 128], F32, kind="ExternalInput").ap(),
        w_v=nc.dram_tensor("w_v", [2, 128, 32, 512], F32, kind="ExternalInput").ap(),
        pos=nc.dram_tensor("pos", [1, T], mybir.dt.int32, kind="ExternalInput").ap(),
        invf=nc.dram_tensor("invf", [128, 1], F32, kind="ExternalInput").ap(),
        rmat=nc.dram_tensor("rmat", [128, 128], F32, kind="ExternalInput").ap(),
        masks=nc.dram_tensor("masks", [128, 4, 512], BF16, kind="ExternalInput").ap(),
        lqk=nc.dram_tensor("lqk", [128, 4], F32, kind="ExternalInput").ap(),
        sub=nc.dram_tensor("sub", [128, 2], F32, kind="ExternalInput").ap(),
        onT=nc.dram_tensor("onT", [HPC * 256, T], BF16, kind="ExternalOutput").ap(),
        uT=nc.dram_tensor("uT", [D, T], BF16, kind="Internal").ap(),
        qk=nc.dram_tensor("qk", [16, 128, T], BF16, kind="Internal").ap(),
        vtm=nc.dram_tensor("vtm", [T, HPC * 256], BF16, kind="Internal").ap(),
    )
    with ExitStack() as es0, nc.allow_low_precision("bf16 matmul operands, fp32 accumulation"):
        P = Prog(nc, es0)
        emit_attn(nc, P, io, lambda_init)
    return nc


def emit_attn(nc_real, P, io, lambda_init, tag="a", do_norm=True):
    T = SEQ
    nc = NS(nc_real, tag)
    xT, nm, w_qk, w_v, pos, invf, rmat, masks, lqk, sub, onT, uT, qk, vtm = (io[k] for k in (
        "xT", "nm", "w_qk", "w_v", "pos", "invf", "rmat", "masks", "lqk", "sub", "onT", "uT", "qk", "vtm"))
    with ExitStack() as es:
        pst = [es.enter_context(nc.psum_tensor("ps%d" % i, [128, 512], F32)) for i in range(8)]
        ones = es.enter_context(nc.sbuf_tensor("ones", [128, 128], BF16))
        onesf = es.enter_context(nc.sbuf_tensor("onesf", [128, 128], F32))
        nmt = es.enter_context(nc.sbuf_tensor("nmt", [128, 32], F32))
        rmt = es.enter_context(nc.sbuf_tensor("rmt", [128, 128], F32))
        ivt = es.enter_context(nc.sbuf_tensor("ivt", [128, 1], F32))
        lqt = es.enter_context(nc.sbuf_tensor("lqt", [128, 4], F32))
        lpt = es.enter_context(nc.sbuf_tensor("lpt", [128, 2], F32))
        nlam = es.enter_context(nc.sbuf_tensor("nlam", [128, 1], F32))
        subt = es.enter_context(nc.sbuf_tensor("subt", [128, 2], F32))
        mkt = es.enter_context(nc.sbuf_tensor("mkt", [128, 4, 512], BF16))
        P.op("dve", lambda e: e.memset(ones[:], 1.0), writes=("ones",))
        P.op("dve", lambda e: e.memset(onesf[:], 1.0), writes=("onesf",))
        for dst, src, tok in ((nmt, nm, "nmt"), (rmt, rmat, "rmt"), (ivt, invf, "ivt"), (lqt, lqk, "lqt"),
                              (subt, sub, "subt"), (mkt, masks, "mkt")):
            P.dma("sp", lambda e, dst=dst, src=src: e.dma_start(out=dst[:], in_=src), writes=(tok,))
        P.op("dve", lambda e: e.tensor_tensor(out=lpt[:, 0:1], in0=lqt[:, 0:1], in1=lqt[:, 1:2], op=ALU.mult),
             reads=("lqt",), writes=("lpt",))
        P.op("dve", lambda e: e.tensor_tensor(out=lpt[:, 1:2], in0=lqt[:, 2:3], in1=lqt[:, 3:4], op=ALU.mult),
             reads=("lqt",), writes=("lpt",))
        P.op("pe", lambda e: e.matmul(pst[7][:, 0:2], lhsT=onesf[:], rhs=lpt[:], start=True, stop=True),
             reads=("lpt", "onesf"), writes=("ps7",))
        P.op("act", lambda e: e.activation(out=lpt[:], in_=pst[7][:, 0:2], func=AF.Exp), reads=("ps7",),
             writes=("lpt",))
        P.op("dve", lambda e: e.tensor_tensor(out=nlam[:], in0=lpt[:, 1:2], in1=lpt[:, 0:1], op=ALU.subtract),
             reads=("lpt",), writes=("nlam",))
        P.op("dve", lambda e: e.tensor_scalar(out=nlam[:], in0=nlam[:], scalar1=-float(lambda_init), scalar2=None,
                                              op0=ALU.add), reads=("nlam",), writes=("nlam",))
        P.op("dve", lambda e: e.tensor_scalar(out=subt[:], in0=subt[:], scalar1=float(1.0 - lambda_init),
                                              scalar2=None, op0=ALU.mult), reads=("subt",), writes=("subt",))

        if do_norm:
            norm_phase(P, nc, xT, nmt, "nmt", uT, ones, pst, T)

        cost = es.enter_context(nc.sbuf_tensor("cost", [128, T], F32))
        sint = es.enter_context(nc.sbuf_tensor("sint", [128, T], F32))
        with ExitStack() as ph:
            posi = ph.enter_context(nc.sbuf_tensor("posi", [128, T], mybir.dt.int32))
            ang = ph.enter_context(nc.sbuf_tensor("ang", [128, T], F32))
            tmp = ph.enter_context(nc.sbuf_tensor("angt", [128, T], F32))
            P.dma("sp", lambda e: e.dma_start(out=posi[:], in_=pos.partition_broadcast(128)), writes=("posi",))
            P.op("dve", lambda e: e.tensor_copy(out=ang[:], in_=posi[:]), reads=("posi",), writes=("ang",))
            P.op("dve", lambda e: e.tensor_scalar(out=ang[:], in0=ang[:], scalar1=ivt[:, 0:1], scalar2=None,
                                                  op0=ALU.mult), reads=("ang", "ivt"), writes=("ang",))
            ki = ph.enter_context(nc.sbuf_tensor("angk", [128, T], mybir.dt.int32))
            TWO_PI = 2.0 * math.pi
            for dst, shift, tok in ((sint, 0.0, "sint"), (cost, 0.5 * math.pi, "cost")):
                P.op("dve", lambda e, shift=shift: e.tensor_scalar(out=tmp[:], in0=ang[:], scalar1=shift,
                                                                   scalar2=None, op0=ALU.add),
                     reads=("ang",), writes=("angt",))
                P.op("dve", lambda e: e.tensor_scalar(out=ki[:], in0=tmp[:], scalar1=1.0 / TWO_PI, scalar2=None,
                                                      op0=ALU.mult), reads=("angt",), writes=("angk",))
                P.op("dve", lambda e, dst=dst: e.tensor_copy(out=dst[:], in_=ki[:]), reads=("angk",), writes=(tok,))
                P.op("dve", lambda e, dst=dst: e.scalar_tensor_tensor(out=tmp[:], in0=dst[:], scalar=-TWO_PI,
                                                                      in1=tmp[:], op0=ALU.mult, op1=ALU.add),
                     reads=(tok, "angt"), writes=("angt",))
                P.op("dve", lambda e, dst=dst: e.tensor_scalar(out=dst[:], in0=tmp[:], scalar1=math.pi,
                                                               scalar2=TWO_PI, op0=ALU.is_gt, op1=ALU.mult),
                     reads=("angt",), writes=(tok,))
                P.op("dve", lambda e, dst=dst: e.tensor_tensor(out=tmp[:], in0=tmp[:], in1=dst[:], op=ALU.subtract),
                     reads=("angt", tok), writes=("angt",))
                P.op("act", lambda e, dst=dst: e.activation(out=dst[:], in_=tmp[:], func=AF.Sin),
                     reads=("angt",), writes=(tok,))
            P.flush()

        with ExitStack() as ph:
            us = Ring([(ph.enter_context(nc.sbuf_tensor("ut%d" % i, [128, 32, 1024], BF16)), "ut%d" % i)
                       for i in range(1)])
            wsl = Ring([(ph.enter_context(nc.sbuf_tensor("wq%d" % i, [128, 32, 128], BF16)), "wq%d" % i)
                        for i in range(2)])
            wvs = Ring([(ph.enter_context(nc.sbuf_tensor("wv%d" % i, [128, 32, 512], BF16)), "wv%d" % i)
                        for i in range(1)])
            qfs = Ring([(ph.enter_context(nc.sbuf_tensor("qf%d" % i, [128, 512], F32)), "qf%d" % i)
                        for i in range(3)])
            t1s = Ring([(ph.enter_context(nc.sbuf_tensor("t1%d" % i, [128, 512], F32)), "t1%d" % i)
                        for i in range(2)])
            qos = Ring([(ph.enter_context(nc.sbuf_tensor("qo%d" % i, [128, 512], BF16)), "qo%d" % i)
                        for i in range(3)])
            vos = Ring([(ph.enter_context(nc.sbuf_tensor("vo%d" % i, [128, 512], BF16)), "vo%d" % i)
                        for i in range(3)])
            psr = Ring([(pst[i], "ps%d" % i) for i in range(4)])
            psrot = Ring([(pst[4 + i], "ps%d" % (4 + i)) for i in range(2)])
            psv = Ring([(pst[6 + i], "ps%d" % (6 + i)) for i in range(2)])
            uv = uT.rearrange("(kc p) t -> p kc t", p=128)
            for t4 in range(T // 1024):
                tb = t4 * 1024
                ut, utok = us.next()
                for hlf in range(2):
                    P.dma("sp", lambda e, ut=ut, tb=tb, hlf=hlf: e.dma_start(
                        out=ut[:, hlf * 16:(hlf + 1) * 16, :], in_=uv[:, hlf * 16:(hlf + 1) * 16, tb:tb + 1024]),
                        writes=(utok,))

                def epi2(bi, ti, tt, ps, ptok, tb=tb):
                    t0, tw = tt
                    g0 = tb + t0
                    qf, qftok = qfs.next()
                    t1, t1tok = t1s.next()
                    qo, qotok = qos.next()
                    pr, prtok = psrot.next()
                    P.op("act", lambda e: e.activation(out=qf[:], in_=ps[:, :], func=AF.Copy), reads=(ptok,),
                         writes=(qftok,))
                    P.op("pe", lambda e: e.matmul(pr[:, :], lhsT=rmt[:], rhs=qf[:], start=True, stop=True),
                         reads=(qftok, "rmt"), writes=(prtok,))
                    P.op("dve", lambda e: e.tensor_tensor(out=t1[:], in0=pr[:, :], in1=sint[:, g0:g0 + 512],
                                                          op=ALU.mult), reads=(prtok, "sint"), writes=(t1tok,))
                    P.op("pool", lambda e: e.tensor_tensor(out=qf[:], in0=qf[:], in1=cost[:, g0:g0 + 512],
                                                           op=ALU.mult), reads=(qftok, "cost"), writes=(qftok,))
                    P.op("dve", lambda e: e.tensor_tensor(out=qo[:], in0=qf[:], in1=t1[:], op=ALU.add),
                         reads=(qftok, t1tok), writes=(qotok,))
                    P.dma("sp", lambda e: e.dma_start(out=qk[bi, :, g0:g0 + 512], in_=qo[:]), reads=(qotok,),
                          writes=(("qk", bi, g0),))

                gemm_T(P, w_qk, 16, 32, ut, utok, [(0, 512), (512, 512)], wsl, psr, epi2)
                for ct in range(2):
                    wv, wvtok = wvs.next()
                    P.dma("pool", lambda e, wv=wv, ct=ct: e.dma_start(out=wv[:], in_=w_v[ct]), writes=(wvtok,))
                    for ts in range(8):
                        ps, ptok = psv.next()
                        vo, votok = vos.next()

                        def mmv(e, ps=ps, wv=wv, ut=ut, ts=ts):
                            inst = None
                            for kc in range(32):
                                inst = e.matmul(ps[:, :], lhsT=ut[:, kc, ts * 128:(ts + 1) * 128], rhs=wv[:, kc, :],
                                                start=(kc == 0), stop=(kc == 31))
                            return inst

                        P.op("pe", mmv, reads=(wvtok, utok), writes=(ptok,))
                        P.op("act", lambda e, vo=vo, ps=ps: e.activation(out=vo[:], in_=ps[:, :], func=AF.Copy),
                             reads=(ptok,), writes=(votok,))
                        r0 = tb + ts * 128
                        P.dma("sp", lambda e, vo=vo, r0=r0, ct=ct: e.dma_start(
                            out=vtm[r0:r0 + 128, ct * 512:(ct + 1) * 512], in_=vo[:]), reads=(votok,),
                            writes=(("vtm", r0, ct),))
            P.flush()

        with ExitStack() as ph:
            qts = Ring([(ph.enter_context(nc.sbuf_tensor("qT%d" % i, [128, 2, T], BF16)), "qT%d" % i)
                        for i in range(2)])
            kts = Ring([(ph.enter_context(nc.sbuf_tensor("kT%d" % i, [128, 2, T], BF16)), "kT%d" % i)
                        for i in range(2)])
            vts = Ring([(ph.enter_context(nc.sbuf_tensor("vT%d" % i, [128, 32, 256], BF16)), "vT%d" % i)
                        for i in range(2)])
            pts = Ring([(ph.enter_context(nc.sbuf_tensor("pT%d" % i, [128, 512], BF16)), "pT%d" % i)
                        for i in range(4)])
            rd = ph.enter_context(nc.sbuf_tensor("rd", [128, 512], F32))
            ot = ph.enter_context(nc.sbuf_tensor("ot", [128, 2, 512], F32))
            o1 = ph.enter_context(nc.sbuf_tensor("o1", [128, 2, 512], F32))
            sqo = ph.enter_context(nc.sbuf_tensor("sqo", [128, 2, 512], BF16))
            rso = ph.enter_context(nc.sbuf_tensor("rso", [128, 512], F32))
            ons = Ring([(ph.enter_context(nc.sbuf_tensor("on%d" % i, [128, 2, 512], BF16)), "on%d" % i)
                        for i in range(2)])
            pss = Ring([(pst[i], "ps%d" % i) for i in range(3)])
            pso = [pst[3], pst[4]]
            psden = pst[5]
            psss = pst[6]
            vv = vtm.rearrange("(kb p) v -> p kb v", p=128)
            for hl in range(HPC):
                qT, qtok = qts.next()
                kT, ktok = kts.next()
                vT, vtok = vts.next()
                for c in range(2):
                    P.dma("sp", lambda e, qT=qT, hl=hl, c=c: e.dma_start(out=qT[:, c, :], in_=qk[hl * 4 + c]),
                          writes=(qtok,))
                    P.dma("sp", lambda e, kT=kT, hl=hl, c=c: e.dma_start(out=kT[:, c, :], in_=qk[hl * 4 + 2 + c]),
                          writes=(ktok,))
                P.dma("sp", lambda e, vT=vT, hl=hl: e.dma_start(out=vT[:], in_=vv[:, :, hl * 256:(hl + 1) * 256]),
                      writes=(vtok,))
                for qt in range(T // 512):
                    q0 = qt * 512
                    nkb = (qt + 1) * 4
                    for c in range(2):
                        def s_op(kb, c=c, q0=q0, qT=qT, kT=kT, qtok=qtok, ktok=ktok):
                            ps, ptok = pss.next()
                            P.op("pe", lambda e: e.matmul(ps[:, :], lhsT=kT[:, c, kb * 128:(kb + 1) * 128],
                                                          rhs=qT[:, c, q0:q0 + 512], start=True, stop=True),
                                 reads=(qtok, ktok), writes=(ptok,))
                            return ps, ptok

                        nxt = s_op(0)
                        for kb in range(nkb):
                            ps, ptok = nxt
                            if kb + 1 < nkb:
                                nxt = s_op(kb + 1)
                            pT, pttok = pts.next()
                            P.op("act", lambda e, pT=pT, ps=ps: e.activation(out=pT[:], in_=ps[:, :], func=AF.Exp,
                                                                             scale=ATT_SCALE),
                                 reads=(ptok,), writes=(pttok,))
                            dj = kb - qt * 4
                            if dj >= 0:
                                P.op("pool", lambda e, pT=pT, dj=dj: e.tensor_tensor(
                                    out=pT[:], in0=pT[:], in1=mkt[:, dj, :], op=ALU.mult),
                                    reads=(pttok, "mkt"), writes=(pttok,))

                            def pv(e, pT=pT, kb=kb, vT=vT, nkb=nkb):
                                e.matmul(pso[0][:, :], lhsT=vT[:, kb, 0:128], rhs=pT[:], start=(kb == 0),
                                         stop=(kb == nkb - 1))
                                e.matmul(pso[1][:, :], lhsT=vT[:, kb, 128:256], rhs=pT[:], start=(kb == 0),
                                         stop=(kb == nkb - 1))
                                return e.matmul(psden[:, :], lhsT=ones[:], rhs=pT[:], start=(kb == 0),
                                                stop=(kb == nkb - 1))

                            P.op("pe", pv, reads=(pttok, vtok, "ones"), writes=("pso",))
                        P.op("dve", lambda e: e.reciprocal(out=rd[:], in_=psden[:, :]), reads=("pso",),
                             writes=("rd",))
                        for v in range(2):
                            if c == 0:
                                P.op("dve", lambda e, v=v: e.tensor_tensor(out=o1[:, v, :], in0=pso[v][:, :],
                                                                           in1=rd[:], op=ALU.mult),
                                     reads=("pso", "rd"), writes=("o1",))
                            else:
                                P.op("dve", lambda e, v=v: e.tensor_tensor(out=ot[:, v, :], in0=pso[v][:, :],
                                                                           in1=rd[:], op=ALU.mult),
                                     reads=("pso", "rd"), writes=("ot",))
                                P.op("dve", lambda e, v=v: e.scalar_tensor_tensor(
                                    out=ot[:, v, :], in0=ot[:, v, :], scalar=nlam[:, 0:1], in1=o1[:, v, :],
                                    op0=ALU.mult, op1=ALU.add), reads=("ot", "o1", "nlam"), writes=("ot",))
                    on, ontok = ons.next()
                    P.op("act", lambda e: e.activation(out=sqo[:], in_=ot[:], func=AF.Square), reads=("ot",),
                         writes=("sqo",))

                    def ssm(e):
                        e.matmul(psss[:, :], lhsT=ones[:], rhs=sqo[:, 0, :], start=True, stop=False)
                        return e.matmul(psss[:, :], lhsT=ones[:], rhs=sqo[:, 1, :], start=False, stop=True)

                    P.op("pe", ssm, reads=("sqo", "ones"), writes=("psss",))
                    P.op("act", lambda e: e.activation(out=rso[:], in_=psss[:, :], func=AF.Sqrt, bias=EPS,
                                                       scale=1.0 / 256.0), reads=("psss",), writes=("rso",))
                    P.op("dve", lambda e: e.reciprocal(out=rso[:], in_=rso[:]), reads=("rso",), writes=("rso",))
                    for v in range(2):
                        P.op("dve", lambda e, v=v, on=on: e.scalar_tensor_tensor(
                            out=on[:, v, :], in0=ot[:, v, :], scalar=subt[:, v:v + 1], in1=rso[:], op0=ALU.mult,
                            op1=ALU.mult), reads=("ot", "rso", "subt"), writes=(ontok,))
                        r0 = (hl * 2 + v) * 128
                        P.dma("sp", lambda e, v=v, on=on, r0=r0, q0=q0: e.dma_start(
                            out=onT[r0:r0 + 128, q0:q0 + 512], in_=on[:, v, :]), reads=(ontok,),
                            writes=(("onT", r0, q0),))
            P.flush()


def rope_consts():
    inv = (1.0 / (10000.0 ** (np.arange(0, 128, 2, dtype=np.float32) / np.float32(128)))).astype(np.float32)
    invf = np.concatenate([inv, inv]).reshape(128, 1).astype(np.float32)
    rm = np.zeros((128, 128), np.float32)
    for dp in range(64):
        rm[dp + 64, dp] = -1.0
    for dp in range(64, 128):
        rm[dp - 64, dp] = 1.0
    k = np.arange(128)[:, None, None]
    j = np.arange(4)[None, :, None]
    q = np.arange(512)[None, None, :]
    masks = ((j * 128 + k) <= q).astype(np.float32).astype(NPBF)
    return invf, rm, masks


def attn_inputs(xT_b, positions_b, r, nmw, w_qkv, lq1, lk1, lq2, lk2, subln):
    invf, rm, masks = rope_consts()
    cols = []
    for hl in range(HPC):
        h = r * HPC + hl
        for kind in range(2):
            for c in range(2):
                c0 = kind * D + h * 256 + c * 128
                cols.append(np.arange(c0, c0 + 128))
    cols = np.concatenate(cols)
    wqk = tile_w(w_qkv[:, cols])
    v0 = 2 * D + r * HPC * 256
    wv = w_qkv[:, v0:v0 + HPC * 256]
    wvt = np.ascontiguousarray(wv.reshape(32, 128, 2, 512).transpose(2, 1, 0, 3))
    return {
        "xT": xT_b, "nm": col_pb(nmw), "w_qk": wqk, "w_v": wvt,
        "pos": np.ascontiguousarray(positions_b.reshape(1, -1).astype(np.int32)),
        "invf": invf, "rmat": rm, "masks": masks,
        "lqk": np.ascontiguousarray(np.stack([lq1, lk1, lq2, lk2], axis=1).astype(np.float32)),
        "sub": col_pb(subln),
    }


HM = 32
NBLK_IN = 37


def bc_last(ap, n):
    return ap.unsqueeze(2).to_broadcast([ap.shape[0], ap.shape[1], n])


def bc_mid(ap, n):
    return ap.unsqueeze(1).to_broadcast([ap.shape[0], n, ap.shape[1]])


def build_mamba():
    T = SEQ
    nc = bass.Bass("TRN2", target_bir_lowering=False)
    io = dict(
        xT=nc.dram_tensor("xT", [D, T], F32, kind="ExternalInput").ap(),
        nm=nc.dram_tensor("nm", [128, 32], F32, kind="ExternalInput").ap(),
        w_in=nc.dram_tensor("w_in", [NBLK_IN, 128, 32, 128], F32, kind="ExternalInput").ap(),
        cwm=nc.dram_tensor("cwm", [128, 20, 4], F32, kind="ExternalInput").ap(),
        cbm=nc.dram_tensor("cbm", [128, 20], F32, kind="ExternalInput").ap(),
        dtb=nc.dram_tensor("dtb", [64, 1], F32, kind="ExternalInput").ap(),
        alc=nc.dram_tensor("alc", [64, 1], F32, kind="ExternalInput").ap(),
        sgn=nc.dram_tensor("sgn", [64, 1], F32, kind="ExternalInput").ap(),
        dcl=nc.dram_tensor("dcl", [128, 16], F32, kind="ExternalInput").ap(),
        nwm=nc.dram_tensor("nwm", [128, 16], F32, kind="ExternalInput").ap(),
        ynT=nc.dram_tensor("ynT", [2048, T], BF16, kind="ExternalOutput").ap(),
        uT=nc.dram_tensor("uT", [D, T], BF16, kind="Internal").ap(),
        szT=nc.dram_tensor("szT", [2048, T], F32, kind="Internal").ap(),
        xbcT=nc.dram_tensor("xbcT", [2560, T], F32, kind="Internal").ap(),
        xcT=nc.dram_tensor("xcT", [2560, T], BF16, kind="Internal").ap(),
        dtaT=nc.dram_tensor("dtaT", [64, T], F32, kind="Internal").ap(),
    )
    with ExitStack() as es0, nc.allow_low_precision("bf16 matmul operands, fp32 accumulation"):
        P = Prog(nc, es0)
        emit_mamba(nc, P, io)
    return nc


def emit_mamba(nc_real, P, io, tag="m", do_norm=True):
    T = SEQ
    nc = NS(nc_real, tag)
    xT, nm, w_in, cwm, cbm, dtb, alc, sgn, dcl, nwm, ynT, uT, szT, xbcT, xcT, dtaT = (io[k] for k in (
        "xT", "nm", "w_in", "cwm", "cbm", "dtb", "alc", "sgn", "dcl", "nwm", "ynT", "uT", "szT", "xbcT", "xcT",
        "dtaT"))
    with ExitStack() as es:
        pst = [es.enter_context(nc.psum_tensor("ps%d" % i, [128, 512], F32)) for i in range(7)]
        psT = es.enter_context(nc.psum_tensor("psT", [128, 1024], BF16))
        ones = es.enter_context(nc.sbuf_tensor("ones", [128, 128], BF16))
        onesf = es.enter_context(nc.sbuf_tensor("onesf", [128, 128], F32))
        nmt = es.enter_context(nc.sbuf_tensor("nmt", [128, 32], F32))
        cwt = es.enter_context(nc.sbuf_tensor("cwt", [128, 20, 4], F32))
        cbt = es.enter_context(nc.sbuf_tensor("cbt", [128, 20], F32))
        dtbt = es.enter_context(nc.sbuf_tensor("dtbt", [64, 1], F32))
        amul = es.enter_context(nc.sbuf_tensor("amul", [64, 1], F32))
        sgnt = es.enter_context(nc.sbuf_tensor("sgnt", [64, 1], F32))
        dct = es.enter_context(nc.sbuf_tensor("dct", [128, 16], F32))
        nwt = es.enter_context(nc.sbuf_tensor("nwt", [128, 16], F32))
        P.op("dve", lambda e: e.memset(ones[:], 1.0), writes=("ones",))
        P.op("dve", lambda e: e.memset(onesf[:], 1.0), writes=("onesf",))
        for dst, src, tok in ((nmt, nm, "nmt"), (cwt, cwm, "cwt"), (cbt, cbm, "cbt"), (dtbt, dtb, "dtbt"),
                              (amul, alc, "amul"), (sgnt, sgn, "sgnt"), (dct, dcl, "dct"), (nwt, nwm, "nwt")):
            P.dma("sp", lambda e, dst=dst, src=src: e.dma_start(out=dst[:], in_=src), writes=(tok,))
        P.op("act", lambda e: e.activation(out=amul[:], in_=amul[:], func=AF.Exp), reads=("amul",), writes=("amul",))
        P.op("dve", lambda e: e.tensor_tensor(out=amul[:], in0=amul[:], in1=sgnt[:], op=ALU.mult),
             reads=("amul", "sgnt"), writes=("amul",))

        if do_norm:
            norm_phase(P, nc, xT, nmt, "nmt", uT, ones, pst, T)

        with ExitStack() as ph:
            us = Ring([(ph.enter_context(nc.sbuf_tensor("ut%d" % i, [128, 32, 1024], BF16)), "ut%d" % i)
                       for i in range(1)])
            wsl = Ring([(ph.enter_context(nc.sbuf_tensor("wi%d" % i, [128, 32, 128], BF16)), "wi%d" % i)
                        for i in range(3)])
            evs = Ring([(ph.enter_context(nc.sbuf_tensor("ev%d" % i, [128, 512], F32)), "ev%d" % i)
                        for i in range(4)])
            psr = Ring([(pst[i], "ps%d" % i) for i in range(6)])
            uv = uT.rearrange("(kc p) t -> p kc t", p=128)
            for t4 in range(T // 1024):
                tb = t4 * 1024
                ut, utok = us.next()
                for hlf in range(2):
                    P.dma("sp", lambda e, ut=ut, tb=tb, hlf=hlf: e.dma_start(
                        out=ut[:, hlf * 16:(hlf + 1) * 16, :], in_=uv[:, hlf * 16:(hlf + 1) * 16, tb:tb + 1024]),
                        writes=(utok,))

                def epi(bi, ti, tt, ps, ptok, tb=tb):
                    t0, tw = tt
                    g0 = tb + t0
                    ev, evtok = evs.next()
                    if bi < 16:
                        P.op("act", lambda e: e.activation(out=ev[:], in_=ps[:, :], func=AF.Silu), reads=(ptok,),
                             writes=(evtok,))
                        P.dma("sp", lambda e: e.dma_start(out=szT[bi * 128:(bi + 1) * 128, g0:g0 + 512], in_=ev[:]),
                              reads=(evtok,), writes=(("szT", bi, g0),))
                    elif bi < 36:
                        j = bi - 16
                        P.op("act", lambda e: e.activation(out=ev[:], in_=ps[:, :], func=AF.Copy), reads=(ptok,),
                             writes=(evtok,))
                        P.dma("sp", lambda e: e.dma_start(out=xbcT[j * 128:(j + 1) * 128, g0:g0 + 512], in_=ev[:]),
                              reads=(evtok,), writes=(("xbcT", j, g0),))
                    else:
                        P.op("act", lambda e: e.activation(out=ev[0:64, :], in_=ps[0:64, :], func=AF.Exp,
                                                           bias=dtbt[:, 0:1]), reads=(ptok, "dtbt"), writes=(evtok,))
                        P.op("act", lambda e: e.activation(out=ev[0:64, :], in_=ev[0:64, :], func=AF.Ln, bias=1.0),
                             reads=(evtok,), writes=(evtok,))
                        P.op("dve", lambda e: e.tensor_scalar(out=ev[0:64, :], in0=ev[0:64, :], scalar1=amul[:, 0:1],
                                                              scalar2=None, op0=ALU.mult),
                             reads=(evtok, "amul"), writes=(evtok,))
                        P.dma("sp", lambda e: e.dma_start(out=dtaT[:, g0:g0 + 512], in_=ev[0:64, :]),
                              reads=(evtok,), writes=(("dtaT", g0),))

                gemm_T(P, w_in, NBLK_IN, 32, ut, utok, [(0, 512), (512, 512)], wsl, psr, epi,
                       mrows=lambda bi: 64 if bi == 36 else 128)
            P.flush()

        with ExitStack() as ph:
            cis = Ring([(ph.enter_context(nc.sbuf_tensor("ci%d" % i, [128, T + 3], F32)), "ci%d" % i)
                        for i in range(2)])
            accs = Ring([(ph.enter_context(nc.sbuf_tensor("ca%d" % i, [128, T], F32)), "ca%d" % i)
                         for i in range(2)])
            cos_ = Ring([(ph.enter_context(nc.sbuf_tensor("co%d" % i, [128, T], BF16)), "co%d" % i)
                         for i in range(2)])
            ptmp = ph.enter_context(nc.sbuf_tensor("ptmp", [128, T], F32))
            for ci, citok in cis.items:
                P.op("dve", lambda e, ci=ci: e.memset(ci[:, 0:3], 0.0), writes=(citok + "h",))
            for j in range(20):
                ci, citok = cis.next()
                acc, atok = accs.next()
                co, cotok = cos_.next()
                veng = "dve" if j % 2 == 0 else "pool"
                P.dma("sp", lambda e, ci=ci, j=j: e.dma_start(out=ci[:, 3:T + 3], in_=xbcT[j * 128:(j + 1) * 128, :]),
                      writes=(citok,))
                P.op(veng, lambda e, ci=ci, acc=acc, j=j: e.tensor_scalar(
                    out=acc[:], in0=ci[:, 0:T], scalar1=cwt[:, j, 0:1], scalar2=cbt[:, j:j + 1], op0=ALU.mult,
                    op1=ALU.add), reads=(citok, citok + "h", "cwt", "cbt"), writes=(atok,))
                for k in (1, 2, 3):
                    if veng == "dve":
                        P.op(veng, lambda e, ci=ci, acc=acc, j=j, k=k: e.scalar_tensor_tensor(
                            out=acc[:], in0=ci[:, k:k + T], scalar=cwt[:, j, k:k + 1], in1=acc[:], op0=ALU.mult,
                            op1=ALU.add), reads=(citok, citok + "h", atok, "cwt"), writes=(atok,))
                    else:
                        P.op(veng, lambda e, ci=ci, j=j, k=k: e.tensor_scalar(
                            out=ptmp[:], in0=ci[:, k:k + T], scalar1=cwt[:, j, k:k + 1], scalar2=None,
                            op0=ALU.mult), reads=(citok, citok + "h", "cwt"), writes=("ptmp",))
                        P.op(veng, lambda e, acc=acc: e.tensor_tensor(out=acc[:], in0=acc[:], in1=ptmp[:],
                                                                      op=ALU.add),
                             reads=(atok, "ptmp"), writes=(atok,))
                P.op("act", lambda e, acc=acc, co=co: e.activation(out=co[:], in_=acc[:], func=AF.Silu),
                     reads=(atok,), writes=(cotok,))
                P.dma("sp", lambda e, co=co, j=j: e.dma_start(out=xcT[j * 128:(j + 1) * 128, :], in_=co[:]),
                      reads=(cotok,), writes=(("xcT", j),))
            P.flush()

        with ExitStack() as ph:
            sb = lambda name, shape, dt: ph.enter_context(nc.sbuf_tensor(name, shape, dt))
            trif = sb("trif", [128, 128], F32)
            ntri = sb("ntri", [128, 128], F32)
            idf = sb("idf", [128, 128], F32)
            idb = sb("idb", [128, 128], BF16)
            nmask = sb("nmask", [128, 4, 128], F32)
            S = sb("S", [128, HM * 64], F32)
            Stmp = sb("Stmp", [128, 1024], F32)
            Sbf = sb("Sbf", [128, HM * 64], BF16)
            elast = sb("elast", [128, HM], F32)
            xcss = Ring([(sb("xcs%d" % i, [128, 20, 512], BF16), "xcs%d" % i) for i in range(2)])
            dtas = Ring([(sb("dta%d" % i, [64, 512], F32), "dta%d" % i) for i in range(2)])
            ysp = sb("ysp", [128, 16, 512], F32)
            szs = sb("szs", [128, 16, 512], F32)
            yno = sb("yno", [128, 16, 512], BF16)
            xtm = sb("xtm", [128, HM, 64], BF16)
            xdt = sb("xdt", [128, HM, 64], BF16)
            xd2 = sb("xd2", [128, HM, 64], BF16)
            btm = sb("btm", [128, 2, 128], BF16)
            dtm = sb("dtm", [128, 64], F32)
            gt = sb("gt", [128, 2, 128], F32)
            rh1 = Ring([(sb("rh1%d" % i, [128, 4, 128], F32), "rh1%d" % i) for i in range(2)])
            abc = Ring([(sb("abc%d" % i, [128, 4, 128], F32), "abc%d" % i) for i in range(2)])
            ets = Ring([(sb("et%d" % i, [128, 4, 128], F32), "et%d" % i) for i in range(2)])
            lts = Ring([(sb("lt%d" % i, [128, 4, 128], BF16), "lt%d" % i) for i in range(2)])
            mts = Ring([(sb("mt%d" % i, [128, 4, 128], BF16), "mt%d" % i) for i in range(2)])
            cds = Ring([(sb("cd%d" % i, [128, 4, 128], BF16), "cd%d" % i) for i in range(2)])
            sqs = Ring([(sb("gsq%d" % i, [128, 512], BF16), "gsq%d" % i) for i in range(2)])
            rsg = sb("rsg", [128, 2, 512], F32)
            psM, psC, psL = pst[0], pst[1], pst[2]
            psYs = Ring([(pst[3], "ps3"), (pst[4], "ps4")])
            psS = [pst[5], pst[6]]

            P.op("pool", lambda e: e.memset(trif[:], 1.0), writes=("trif",))
            P.op("pool", lambda e: e.affine_select(out=trif[:], in_=trif[:], pattern=[[1, 128]], compare_op=ALU.is_ge,
                                                   fill=0.0, base=0, channel_multiplier=-1),
                 reads=("trif",), writes=("trif",))
            P.op("pool", lambda e: e.memset(ntri[:], -1.0), writes=("ntri",))
            P.op("pool", lambda e: e.affine_select(out=ntri[:], in_=ntri[:], pattern=[[1, 128]], compare_op=ALU.is_ge,
                                                   fill=0.0, base=0, channel_multiplier=-1),
                 reads=("ntri",), writes=("ntri",))
            P.op("pool", lambda e: e.memset(idf[:], 1.0), writes=("idf",))
            P.op("pool", lambda e: e.affine_select(out=idf[:], in_=idf[:], pattern=[[-1, 128]],
                                                   compare_op=ALU.is_equal, fill=0.0, base=0, channel_multiplier=1),
                 reads=("idf",), writes=("idf",))
            P.op("dve", lambda e: e.tensor_copy(out=idb[:], in_=idf[:]), reads=("idf",), writes=("idb",))
            P.op("pool", lambda e: e.memset(nmask[:], -1e30), writes=("nmask",))
            P.op("pool", lambda e: e.affine_select(out=nmask[:], in_=nmask[:], pattern=[[0, 4], [-1, 128]],
                                                   compare_op=ALU.is_gt, fill=0.0, base=0, channel_multiplier=1),
                 reads=("nmask",), writes=("nmask",))
            P.op("dve", lambda e: e.memset(S[:], 0.0), writes=("S",))
            P.op("dve", lambda e: e.memset(Sbf[:], 0.0), writes=("Sbf",))

            xcv = xcT.rearrange("(j p) t -> p j t", p=128)
            szv = szT.rearrange("(j p) t -> p j t", p=128)
            ynv = ynT.rearrange("(j p) t -> p j t", p=128)
            for sp_i in range(T // 512):
                s0 = sp_i * 512
                xcs, xcstok = xcss.next()
                dta, dtatok = dtas.next()
                P.dma("sp", lambda e, xcs=xcs, s0=s0: e.dma_start(out=xcs[:], in_=xcv[:, :, s0:s0 + 512]),
                      writes=(xcstok,))
                P.dma("sp", lambda e, dta=dta, s0=s0: e.dma_start(out=dta[:], in_=dtaT[:, s0:s0 + 512]),
                      writes=(dtatok,))
                for cc in range(4):
                    c0 = cc * 128
                    for half in range(2):
                        def trx(e, half=half, xcs=xcs, c0=c0):
                            inst = None
                            for jj in range(8):
                                inst = e.transpose(psT[:, jj * 128:(jj + 1) * 128], xcs[:, half * 8 + jj, c0:c0 + 128],
                                                   idb[:])
                            return inst
                        P.op("pe", trx, reads=(xcstok, "idb"), writes=("psT",))
                        P.op("act", lambda e, half=half: e.activation(
                            out=xtm[:, half * 16:(half + 1) * 16, :].rearrange("p h q -> p (h q)"), in_=psT[:, :],
                            func=AF.Copy), reads=("psT",), writes=("xtm",))

                    def trb(e, xcs=xcs, c0=c0):
                        e.transpose(psT[:, 0:128], xcs[:, 16, c0:c0 + 128], idb[:])
                        return e.transpose(psT[:, 128:256], xcs[:, 17, c0:c0 + 128], idb[:])
                    P.op("pe", trb, reads=(xcstok, "idb"), writes=("psT",))
                    P.op("act", lambda e: e.activation(out=btm[:].rearrange("p g n -> p (g n)"), in_=psT[:, 0:256],
                                                       func=AF.Copy), reads=("psT",), writes=("btm",))
                    P.op("pe", lambda e, dta=dta, c0=c0: e.transpose(psM[:, 0:64], dta[0:64, c0:c0 + 128],
                                                                     idf[0:64, 0:64]),
                         reads=(dtatok, "idf"), writes=("psM",))
                    P.op("act", lambda e: e.activation(out=dtm[:], in_=psM[:, 0:64], func=AF.Copy), reads=("psM",),
                         writes=("dtm",))
                    def gmm(e, xcs=xcs, c0=c0):
                        e.matmul(psM[:, 128:256], lhsT=xcs[:, 16, c0:c0 + 128], rhs=xcs[:, 18, c0:c0 + 128],
                                 start=True, stop=True)
                        return e.matmul(psM[:, 256:384], lhsT=xcs[:, 17, c0:c0 + 128], rhs=xcs[:, 19, c0:c0 + 128],
                                        start=True, stop=True)
                    P.op("pe", gmm, reads=(xcstok, "dtm"), writes=("psM",))
                    P.op("act", lambda e: e.activation(out=gt[:].rearrange("p g n -> p (g n)"), in_=psM[:, 128:384],
                                                       func=AF.Copy), reads=("psM",), writes=("gt",))
                    P.op("dve", lambda e: e.tensor_tensor(out=xdt[:], in0=xtm[:], in1=bc_last(dtm[:, 0:32], 64),
                                                          op=ALU.mult), reads=("xtm", "dtm"), writes=("xdt",))
                    for q in range(8):
                        h0 = q * 4
                        g = q // 4
                        r1, r1tok = rh1.next()
                        ab, abtok = abc.next()
                        et, ettok = ets.next()
                        lt, lttok = lts.next()
                        mt, mttok = mts.next()
                        cd, cdtok = cds.next()
                        P.op("dve", lambda e, r1=r1, h0=h0: e.tensor_tensor(
                            out=r1[:], in0=bc_mid(trif[:, :], 4), in1=bc_last(dtm[:, 32 + h0:32 + h0 + 4], 128),
                            op=ALU.mult), reads=("trif", "dtm"), writes=(r1tok,))
                        P.op("pool", lambda e, ab=ab, h0=h0: e.tensor_copy(
                            out=ab[:], in_=bc_last(dtm[:, 32 + h0:32 + h0 + 4], 128)), reads=("dtm",),
                            writes=(abtok,))
                        P.op("pe", lambda e, r1=r1: e.matmul(psC[:, :], lhsT=onesf[:],
                                                             rhs=r1[:].rearrange("p h l -> p (h l)"), start=True,
                                                             stop=True), reads=(r1tok, "onesf"), writes=("psC",))

                        def lmm(e, r1=r1, ab=ab):
                            e.matmul(psL[:, :], lhsT=onesf[:], rhs=r1[:].rearrange("p h l -> p (h l)"), start=True,
                                     stop=False)
                            e.matmul(psL[:, :], lhsT=ntri[:], rhs=ab[:].rearrange("p h l -> p (h l)"), start=False,
                                     stop=False)
                            return e.matmul(psL[:, :], lhsT=idf[:], rhs=nmask[:].rearrange("p h l -> p (h l)"),
                                            start=False, stop=True)
                        P.op("pe", lmm, reads=(r1tok, abtok, "onesf", "ntri", "idf", "nmask"), writes=("psL",))
                        P.op("act", lambda e, et=et: e.activation(out=et[:].rearrange("p h l -> p (h l)"),
                                                                  in_=psC[:, :], func=AF.Exp),
                             reads=("psC",), writes=(ettok,))
                        P.op("act", lambda e, lt=lt: e.activation(out=lt[:].rearrange("p h l -> p (h l)"),
                                                                  in_=psL[:, :], func=AF.Exp),
                             reads=("psL",), writes=(lttok,))
                        P.op("pool", lambda e, et=et, h0=h0: e.tensor_copy(out=elast[:, h0:h0 + 4], in_=et[:, :, 127]),
                             reads=(ettok,), writes=("elast",))
                        P.op("dve", lambda e, mt=mt, lt=lt, g=g: e.tensor_tensor(
                            out=mt[:], in0=lt[:], in1=bc_mid(gt[:, g, :], 4), op=ALU.mult),
                            reads=(lttok, "gt"), writes=(mttok,))
                        P.op("pool", lambda e, cd=cd, et=et, g=g, xcs=xcs, c0=c0: e.tensor_tensor(
                            out=cd[:], in0=et[:], in1=bc_mid(xcs[:, 18 + g, c0:c0 + 128], 4), op=ALU.mult),
                            reads=(ettok, xcstok), writes=(cdtok,))
                        P.op("dve", lambda e, lt=lt, h0=h0: e.tensor_tensor(
                            out=xd2[:, h0:h0 + 4, :], in0=xdt[:, h0:h0 + 4, :], in1=bc_last(lt[:, :, 127], 64),
                            op=ALU.mult), reads=(lttok, "xdt"), writes=("xd2",))
                        if q % 2 == 0:
                            psY, ytok = psYs.next()

                        def ymm(e, mt=mt, cd=cd, h0=h0, psY=psY):
                            inst = None
                            for hh in range(4):
                                h = h0 + hh
                                pr = (h % 2) * 64
                                cl = ((h // 2) % 4) * 128
                                e.matmul(psY[pr:pr + 64, cl:cl + 128], lhsT=xdt[:, h, :], rhs=mt[:, hh, :],
                                         start=True, stop=False)
                                inst = e.matmul(psY[pr:pr + 64, cl:cl + 128], lhsT=Sbf[:, h * 64:(h + 1) * 64],
                                                rhs=cd[:, hh, :], start=False, stop=True)
                            return inst
                        P.op("pe", ymm, reads=(mttok, cdtok, "xdt", "Sbf"), writes=(ytok,))
                        if q % 2 == 1:
                            for jj in range(4):
                                j = (q // 2) * 4 + jj
                                P.op("dve", lambda e, j=j, jj=jj, psY=psY, xcs=xcs, c0=c0: e.scalar_tensor_tensor(
                                    out=ysp[:, j, c0:c0 + 128], in0=xcs[:, j, c0:c0 + 128], scalar=dct[:, j:j + 1],
                                    in1=psY[:, jj * 128:(jj + 1) * 128], op0=ALU.mult, op1=ALU.add),
                                    reads=(ytok, xcstok, "dct"), writes=("ysp",))
                    for g in range(2):
                        def smm(e, g=g):
                            inst = None
                            for i in range(2):
                                inst = e.matmul(psS[i][:, :], lhsT=btm[:, g, :],
                                                rhs=xd2[:, g * 16 + i * 8:g * 16 + i * 8 + 8, :].rearrange(
                                                    "p h q -> p (h q)"), start=True, stop=True)
                            return inst
                        P.op("pe", smm, reads=("btm", "xd2"), writes=("psS",))
                        P.op("dve", lambda e, g=g: e.tensor_tensor(
                            out=Stmp[:].rearrange("p (h q) -> p h q", q=64),
                            in0=S[:, g * 1024:(g + 1) * 1024].rearrange("p (h q) -> p h q", q=64),
                            in1=bc_last(elast[:, g * 16:(g + 1) * 16], 64), op=ALU.mult),
                            reads=("S", "elast"), writes=("Stmp",))
                        for i in range(2):
                            P.op("dve", lambda e, g=g, i=i: e.tensor_tensor(
                                out=S[:, g * 1024 + i * 512:g * 1024 + (i + 1) * 512],
                                in0=Stmp[:, i * 512:(i + 1) * 512], in1=psS[i][:, :], op=ALU.add),
                                reads=("Stmp", "psS"), writes=("S",))
                        P.op("pool", lambda e, g=g: e.tensor_copy(out=Sbf[:, g * 1024:(g + 1) * 1024],
                                                                  in_=S[:, g * 1024:(g + 1) * 1024]),
                             reads=("S",), writes=("Sbf",))
                P.dma("sp", lambda e, s0=s0: e.dma_start(out=szs[:], in_=szv[:, :, s0:s0 + 512]), writes=("szs",))
                for j in range(16):
                    g = j // 8
                    sq, sqtok = sqs.next()
                    P.op("dve", lambda e, j=j: e.tensor_tensor(out=ysp[:, j, :], in0=ysp[:, j, :], in1=szs[:, j, :],
                                                               op=ALU.mult), reads=("ysp", "szs"), writes=("ysp",))
                    P.op("act", lambda e, j=j, sq=sq: e.activation(out=sq[:], in_=ysp[:, j, :], func=AF.Square),
                         reads=("ysp",), writes=(sqtok,))
                    P.op("pe", lambda e, j=j, sq=sq, g=g: e.matmul((psC if g == 0 else psL)[:, :], lhsT=ones[:],
                                                                   rhs=sq[:], start=(j % 8 == 0), stop=(j % 8 == 7)),
                         reads=(sqtok, "ones"), writes=("psC" if g == 0 else "psL",))
                for g in range(2):
                    P.op("act", lambda e, g=g: e.activation(out=rsg[:, g, :], in_=(psC if g == 0 else psL)[:, :],
                                                            func=AF.Sqrt, bias=EPS, scale=1.0 / 1024.0),
                         reads=("psC" if g == 0 else "psL",), writes=("rsg",))
                P.op("dve", lambda e: e.reciprocal(out=rsg[:], in_=rsg[:]), reads=("rsg",), writes=("rsg",))
                for j in range(16):
                    P.op("dve", lambda e, j=j: e.scalar_tensor_tensor(
                        out=yno[:, j, :], in0=ysp[:, j, :], scalar=nwt[:, j:j + 1], in1=rsg[:, j // 8, :],
                        op0=ALU.mult, op1=ALU.mult), reads=("ysp", "rsg", "nwt"), writes=("yno",))
                P.dma("sp", lambda e, s0=s0: e.dma_start(out=ynv[:, :, s0:s0 + 512], in_=yno[:]), reads=("yno",),
                      writes=(("ynT", s0),))
            P.flush()


def mamba_inputs(xT_b, r, nmw, w_in, conv_w, conv_b, dt_bias, a_log, d_skip, norm_w):
    GN = 8 * 128
    zc = np.arange(r * 2048, (r + 1) * 2048)
    xc = DIN + zc
    bcn = 2 * DIN + np.arange(r * 256, (r + 1) * 256)
    ccn = 2 * DIN + GN + np.arange(r * 256, (r + 1) * 256)
    dc = 2 * DIN + 2 * GN + np.arange(r * 32, (r + 1) * 32)
    wmain = w_in[:, np.concatenate([zc, xc, bcn, ccn])]
    wdt = np.zeros((D, 128), np.float32)
    wdt[:, 0:32] = w_in[:, dc]
    wdt[:, 32:64] = w_in[:, dc]
    wt = tile_w(np.concatenate([wmain, wdt], axis=1))
    cch = np.concatenate([zc, DIN + np.arange(r * 256, (r + 1) * 256), DIN + GN + np.arange(r * 256, (r + 1) * 256)])
    cwm = np.ascontiguousarray(conv_w[:, cch].T.reshape(20, 128, 4).transpose(1, 0, 2))
    cbm = col_pb(conv_b[cch])
    hs = np.arange(r * 32, (r + 1) * 32)
    dtb = np.concatenate([dt_bias[hs], dt_bias[hs]]).reshape(64, 1).astype(np.float32)
    alc = np.concatenate([np.zeros(32, np.float32), a_log[hs]]).reshape(64, 1).astype(np.float32)
    sgn = np.concatenate([np.ones(32, np.float32), -np.ones(32, np.float32)]).reshape(64, 1)
    dcl = col_pb(np.repeat(d_skip[hs], 64))
    nwm = col_pb(norm_w[zc])
    return {"xT": xT_b, "nm": col_pb(nmw), "w_in": wt, "cwm": cwm, "cbm": cbm, "dtb": dtb, "alc": alc, "sgn": sgn,
            "dcl": dcl, "nwm": nwm}


TH = SEQ + HALO


def build_full(layers=(0, 1, 2, 3), final=True):
    T = SEQ
    nc = bass.Bass("TRN2", target_bir_lowering=False)
    ext = lambda name, shape, dt=F32: nc.dram_tensor(name, shape, dt, kind="ExternalInput").ap()
    itn = lambda name, shape, dt=F32: nc.dram_tensor(name, shape, dt, kind="Internal").ap()
    x0 = ext("x0", [D, TH])
    pos = ext("pos", [1, T], mybir.dt.int32)
    invf = ext("invf", [128, 1])
    rmat = ext("rmat", [128, 128])
    masks = ext("masks", [128, 4, 512], BF16)
    sgn = ext("sgn", [64, 1])
    fin = ext("fin", [128, 32])
    W = {}
    for i in layers:
        p = "l%d_" % i
        W[p + "nm"] = ext(p + "nm", [128, 32])
        if i % 2 == 0:
            W[p + "w_in"] = ext(p + "w_in", [4, NBLK_IN, 128, 32, 128])
            W[p + "cwm"] = ext(p + "cwm", [4, 128, 20, 4])
            W[p + "cbm"] = ext(p + "cbm", [4, 128, 20])
            W[p + "dtb"] = ext(p + "dtb", [4, 64, 1])
            W[p + "alc"] = ext(p + "alc", [4, 64, 1])
            W[p + "dcl"] = ext(p + "dcl", [4, 128, 16])
            W[p + "nwm"] = ext(p + "nwm", [4, 128, 16])
            KCo = 64
        else:
            W[p + "w_qk"] = ext(p + "w_qk", [4, 16, 128, 32, 128])
            W[p + "w_v"] = ext(p + "w_v", [4, 2, 128, 32, 512])
            W[p + "lqk"] = ext(p + "lqk", [128, 4])
            W[p + "sub"] = ext(p + "sub", [128, 2])
            KCo = 32
        W[p + "w_o"] = ext(p + "w_o", [32, 128, KCo, 128])
        W[p + "nf"] = ext(p + "nf", [128, 32])
        W[p + "w_up"] = ext(p + "w_up", [2 * NFB, 128, 32, 128])
        W[p + "cw"] = ext(p + "cw", [128, 2 * NFB, 3])
        W[p + "cb"] = ext(p + "cb", [128, 2 * NFB])
        for g in range(4):
            W[p + "w_dn%d" % g] = ext(p + "w_dn%d" % g, [32, 128, GROUPS[g], 128])
    out = nc.dram_tensor("out", [D, T], F32, kind="ExternalOutput").ap()
    xa = itn("xa", [D, TH])
    xb = itn("xb", [D, TH])
    ynT = itn("ynT", [DIN, TH], BF16)
    uT = itn("uT", [D, T], BF16)
    szT = itn("szT", [2048, T])
    xbcT = itn("xbcT", [2560, T])
    xcT = itn("xcT", [2560, T], BF16)
    dtaT = itn("dtaT", [64, T])
    qk = itn("qk", [16, 128, T], BF16)
    vtm = itn("vtm", [T, HPC * 256], BF16)
    xm = itn("xm", [D, TLH])

    with ExitStack() as es0, nc.allow_low_precision("bf16 matmul operands, fp32 accumulation"):
        P = Prog(nc, es0)
        with ExitStack() as es:
            zf = es.enter_context(nc.sbuf_tensor("zf", [128, 64, HALO], F32))
            zb = es.enter_context(nc.sbuf_tensor("zb", [128, 64, HALO], BF16))
            P.op("dve", lambda e: e.memset(zf[:], 0.0), writes=("zf",))
            P.op("dve", lambda e: e.memset(zb[:], 0.0), writes=("zb",))
            for nm_, buf in (("xa", xa), ("xb", xb)):
                P.dma("sp", lambda e, buf=buf: e.dma_start(
                    out=buf[:, 0:HALO].rearrange("(k p) t -> p k t", p=128), in_=zf[:, 0:32, :]),
                    reads=("zf",), writes=(nm_,))
            P.dma("sp", lambda e: e.dma_start(out=ynT[:, 0:HALO].rearrange("(k p) t -> p k t", p=128), in_=zb[:]),
                  reads=("zb",), writes=("ynT",))
            P.flush()
        cur = x0
        ping = [xb, xa]
        for li, i in enumerate(layers):
            p = "l%d_" % i
            last = (li == len(layers) - 1)
            nxt = ping[li % 2]
            xfull = cur[:, HALO:TH]
            if i % 2 == 0:
                KCo = 64
                for r in range(4):
                    io = dict(xT=xfull, nm=W[p + "nm"], w_in=W[p + "w_in"][r], cwm=W[p + "cwm"][r],
                              cbm=W[p + "cbm"][r], dtb=W[p + "dtb"][r], alc=W[p + "alc"][r], sgn=sgn,
                              dcl=W[p + "dcl"][r], nwm=W[p + "nwm"][r],
                              ynT=ynT[r * 2048:(r + 1) * 2048, HALO:TH], uT=uT, szT=szT, xbcT=xbcT, xcT=xcT,
                              dtaT=dtaT)
                    emit_mamba(nc, P, io, do_norm=(r == 0))
            else:
                KCo = 32
                lam_init = 0.8 - 0.6 * math.exp(-0.3 * i)
                for r in range(4):
                    io = dict(xT=xfull, nm=W[p + "nm"], w_qk=W[p + "w_qk"][r], w_v=W[p + "w_v"][r], pos=pos,
                              invf=invf, rmat=rmat, masks=masks, lqk=W[p + "lqk"], sub=W[p + "sub"],
                              onT=ynT[r * 1024:(r + 1) * 1024, HALO:TH], uT=uT, qk=qk, vtm=vtm)
                    emit_attn(nc, P, io, lam_init, do_norm=(r == 0))
            for r in range(4):
                c0 = r * TL
                dst = out[:, c0:c0 + TL] if last else nxt[:, HALO + c0:HALO + c0 + TL]
                io = dict(xT=cur[:, c0:c0 + TLH], ynT=ynT[0:KCo * 128, c0:c0 + TLH], w_o=W[p + "w_o"],
                          nf=W[p + "nf"], w_up=W[p + "w_up"], cw=W[p + "cw"], cb=W[p + "cb"],
                          w_dn=[W[p + "w_dn%d" % g] for g in range(4)], fin=fin, out=dst, xm=xm)
                emit_row(nc, P, io, KCo, final and last)
            cur = nxt
    return nc


def full_inputs(inp, layers=(0, 1, 2, 3)):
    f32 = lambda a: np.ascontiguousarray(np.asarray(a), dtype=np.float32)
    invf, rm, masks = rope_consts()
    base = {"invf": invf, "rmat": rm, "masks": masks,
            "sgn": np.concatenate([np.ones(32, np.float32), -np.ones(32, np.float32)]).reshape(64, 1),
            "fin": col_pb(f32(inp["final_norm"]))}
    for i in layers:
        p = "l%d_" % i
        nmw = f32(inp[p + "norm_mix"])
        base[p + "nm"] = col_pb(nmw)
        if i % 2 == 0:
            w_in = f32(inp[p + "m_w_in"])
            args = [f32(inp[p + k]) for k in ("m_conv_w", "m_conv_b", "m_dt_bias", "m_a_log", "m_d", "m_norm")]
            per_r = [mamba_inputs(None, r, nmw, w_in, *args) for r in range(4)]
            for src, dst in (("w_in", "w_in"), ("cwm", "cwm"), ("cbm", "cbm"), ("dtb", "dtb"), ("alc", "alc"),
                             ("dcl", "dcl"), ("nwm", "nwm")):
                base[p + dst] = np.ascontiguousarray(np.stack([per_r[r][src] for r in range(4)], axis=0))
            w_out = f32(inp[p + "m_w_out"])
            del w_in, per_r
        else:
            w_qkv = f32(inp[p + "a_w_qkv"])
            args = [f32(inp[p + k]) for k in ("a_lq1", "a_lk1", "a_lq2", "a_lk2", "a_subln")]
            per_r = [attn_inputs(None, np.zeros(SEQ, np.int32), r, nmw, w_qkv, *args) for r in range(4)]
            base[p + "w_qk"] = np.ascontiguousarray(np.stack([per_r[r]["w_qk"] for r in range(4)], axis=0))
            base[p + "w_v"] = np.ascontiguousarray(np.stack([per_r[r]["w_v"] for r in range(4)], axis=0))
            base[p + "lqk"] = per_r[0]["lqk"]
            base[p + "sub"] = per_r[0]["sub"]
            w_out = f32(inp[p + "a_w_o"])
            del w_qkv, per_r
        ri = row_inputs(np.zeros((1, 1), np.float32), np.zeros((1, 1), NPBF), w_out, f32(inp[p + "norm_ffn"]),
                        f32(inp[p + "f_w_up"]), f32(inp[p + "f_conv_w"]), f32(inp[p + "f_conv_b"]),
                        f32(inp[p + "f_w_down"]), f32(inp["final_norm"]))
        for k in ("w_o", "nf", "w_up", "cw", "cb", "w_dn0", "w_dn1", "w_dn2", "w_dn3"):
            base[p + k] = ri[k]
        del ri, w_out
    x = f32(inp["x"])
    positions = np.asarray(inp["positions"]).astype(np.int32)
    maps = []
    for b in range(x.shape[0]):
        m = dict(base)
        x0 = np.zeros((D, TH), np.float32)
        x0[:, HALO:] = x[b].T
        m["x0"] = x0
        m["pos"] = np.ascontiguousarray(positions[b].reshape(1, -1))
        maps.append(m)
    return maps


INPUT_NAMES = (
    "x",
    "positions",
    "l0_norm_mix",
    "l0_m_w_in",
    "l0_m_conv_w",
    "l0_m_conv_b",
    "l0_m_dt_bias",
    "l0_m_a_log",
    "l0_m_d",
    "l0_m_norm",
    "l0_m_w_out",
    "l0_norm_ffn",
    "l0_f_w_up",
    "l0_f_conv_w",
    "l0_f_conv_b",
    "l0_f_w_down",
    "l1_norm_mix",
    "l1_a_w_qkv",
    "l1_a_lq1",
    "l1_a_lk1",
    "l1_a_lq2",
    "l1_a_lk2",
    "l1_a_subln",
    "l1_a_w_o",
    "l1_norm_ffn",
    "l1_f_w_up",
    "l1_f_conv_w",
    "l1_f_conv_b",
    "l1_f_w_down",
    "l2_norm_mix",
    "l2_m_w_in",
    "l2_m_conv_w",
    "l2_m_conv_b",
    "l2_m_dt_bias",
    "l2_m_a_log",
    "l2_m_d",
    "l2_m_norm",
    "l2_m_w_out",
    "l2_norm_ffn",
    "l2_f_w_up",
    "l2_f_conv_w",
    "l2_f_conv_b",
    "l2_f_w_down",
    "l3_norm_mix",
    "l3_a_w_qkv",
    "l3_a_lq1",
    "l3_a_lk1",
    "l3_a_lq2",
    "l3_a_lk2",
    "l3_a_subln",
    "l3_a_w_o",
    "l3_norm_ffn",
    "l3_f_w_up",
    "l3_f_conv_w",
    "l3_f_conv_b",
    "l3_f_w_down",
    "final_norm",
)

_NC_FULL = {}


def kernel(**inp):
    missing = [n for n in INPUT_NAMES if n not in inp]
    assert not missing, "missing inputs: %s" % missing
    if "full" not in _NC_FULL:
        _NC_FULL["full"] = build_full()
    nc = _NC_FULL["full"]
    maps = full_inputs(inp)
    res = run_bass_kernel_spmd(nc, maps, core_ids=list(range(len(maps))))
    out = np.stack([res.results[b]["out"].T for b in range(len(maps))], axis=0)
    return np.ascontiguousarray(out, dtype=np.float32)
```
